# Optimizing a Trainium2 kernel written in Bass

```python
import math
import jax, jax.numpy as jnp
from jax import lax
import numpy as np

D_MODEL = 1024
BATCH = 4
SEQ = 4096
DEPTH = 2

N_A_LAYERS = DEPTH // 2
N_B_LAYERS = DEPTH - N_A_LAYERS
HEAD_DIM = 64
N_HEADS = D_MODEL // (2 * HEAD_DIM)
D_FF = 2816
CONV_WIDTH = 3
Q_BLOCK = 128
NORM_EPS = 1e-6

kernel_name = "yoco_shortconv_diffattn_macaron"


def rmsnorm(x, g):
    xf = x.astype(jnp.float32)
    y = xf * lax.rsqrt(jnp.mean(xf * xf, axis=-1, keepdims=True) + NORM_EPS)
    return (y * g.astype(jnp.float32)).astype(x.dtype)


def swiglu(x, w_gu, w_down):
    gate, up = jnp.split(x @ w_gu, 2, axis=-1)
    return (jax.nn.silu(gate) * up) @ w_down


def short_gated_conv(xn, w_in, conv_k, w_out):
    b_gate, c_gate, z = jnp.split(xn @ w_in, 3, axis=-1)
    u = c_gate * z
    rhs = conv_k[:, None, :].astype(u.dtype)
    conv = lax.conv_general_dilated(
        u, rhs, window_strides=(1,), padding=[(CONV_WIDTH - 1, 0)],
        dimension_numbers=('NWC', 'WIO', 'NWC'), feature_group_count=u.shape[-1])
    return (b_gate * conv) @ w_out


def alibi_slopes(n_heads):
    return 2.0 ** (-8.0 * jnp.arange(1, n_heads + 1, dtype=jnp.float32) / n_heads)


def shared_kv(h, kv_norm_g, w_kv):
    Bsz, S, _ = h.shape
    k_flat, v_flat = jnp.split(rmsnorm(h, kv_norm_g) @ w_kv, 2, axis=-1)
    k = k_flat.reshape(Bsz, S, N_HEADS, 2, HEAD_DIM)
    v = v_flat.reshape(Bsz, S, N_HEADS, 2 * HEAD_DIM)
    return k, v


def diff_attention(xn, k, v, w_q, lam_params, subln_g, w_o, lambda_init):
    Bsz, S, _ = xn.shape
    q = (xn @ w_q).reshape(Bsz, S, N_HEADS, 2, HEAD_DIM)
    lp = lam_params.astype(jnp.float32)
    lam = jnp.exp(jnp.sum(lp[0] * lp[1])) - jnp.exp(jnp.sum(lp[2] * lp[3])) + lambda_init
    scale = HEAD_DIM ** -0.5
    n_blocks = S // Q_BLOCK
    q_blocks = q.reshape(Bsz, n_blocks, Q_BLOCK, N_HEADS, 2, HEAD_DIM).swapaxes(0, 1)
    slopes = alibi_slopes(N_HEADS)
    kpos = jnp.arange(S)

    def one_block(args):
        q_blk, start = args
        qpos = start + jnp.arange(Q_BLOCK)
        dist = (qpos[:, None] - kpos[None, :]).astype(jnp.float32)
        s = jnp.einsum('bqhcd,bkhcd->bhcqk', q_blk, k,
                       preferred_element_type=jnp.float32) * scale
        s = s - slopes[None, :, None, None, None] * dist
        s = jnp.where(dist >= 0, s, -jnp.inf)
        p = jax.nn.softmax(s, axis=-1)
        a = p[:, :, 0] - lam * p[:, :, 1]
        return jnp.einsum('bhqk,bkhe->bqhe', a, v)

    o = lax.map(one_block, (q_blocks, jnp.arange(n_blocks) * Q_BLOCK))
    o = o.swapaxes(0, 1).reshape(Bsz, S, N_HEADS, 2 * HEAD_DIM).astype(jnp.float32)
    o = o * lax.rsqrt(jnp.mean(o * o, axis=-1, keepdims=True) + NORM_EPS)
    o = o * subln_g.astype(jnp.float32) * (1.0 - lambda_init)
    return o.reshape(Bsz, S, N_HEADS * 2 * HEAD_DIM).astype(xn.dtype) @ w_o


def setup_inputs(seed: int = 0) -> dict:
    key = jax.random.key(seed)
    ks = jax.random.split(key, 16)
    D = D_MODEL
    nrm = lambda k, shape, fan_in: jax.random.normal(k, shape, jnp.float32) * fan_in ** -0.5
    x = jax.random.normal(ks[0], (BATCH, SEQ, D), jnp.float32)
    ffn_w_gu = nrm(ks[1], (DEPTH, 2, D, 2 * D_FF), D)
    ffn_w_down = nrm(ks[2], (DEPTH, 2, D_FF, D), D_FF)
    norm_g = 1.0 + 0.05 * jax.random.normal(ks[3], (DEPTH, 6, D), jnp.float32)
    conv_w_in = nrm(ks[4], (N_A_LAYERS, D, 3 * D), D)
    conv_k = nrm(ks[5], (N_A_LAYERS, CONV_WIDTH, D), CONV_WIDTH)
    conv_w_out = nrm(ks[6], (N_A_LAYERS, D, D), D)
    kv_norm_g = 1.0 + 0.05 * jax.random.normal(ks[7], (D,), jnp.float32)
    w_kv = nrm(ks[8], (D, 2 * D), D)
    attn_w_q = nrm(ks[9], (N_B_LAYERS, D, D), D)
    attn_lambda = 0.1 * jax.random.normal(ks[10], (N_B_LAYERS, 4, HEAD_DIM), jnp.float32)
    attn_subln_g = 1.0 + 0.05 * jax.random.normal(ks[11], (N_B_LAYERS, 2 * HEAD_DIM), jnp.float32)
    attn_w_o = nrm(ks[12], (N_B_LAYERS, D, D), D)
    return {"x": x, "ffn_w_gu": ffn_w_gu, "ffn_w_down": ffn_w_down, "norm_g": norm_g,
            "conv_w_in": conv_w_in, "conv_k": conv_k, "conv_w_out": conv_w_out,
            "kv_norm_g": kv_norm_g, "w_kv": w_kv, "attn_w_q": attn_w_q,
            "attn_lambda": attn_lambda, "attn_subln_g": attn_subln_g, "attn_w_o": attn_w_o}


def reference(x, ffn_w_gu, ffn_w_down, norm_g, conv_w_in, conv_k, conv_w_out,
              kv_norm_g, w_kv, attn_w_q, attn_lambda, attn_subln_g, attn_w_o):
    h = x
    k = v = None
    for l in range(DEPTH):
        g = norm_g[l]
        if l == N_A_LAYERS:
            k, v = shared_kv(h, kv_norm_g, w_kv)
        h = h + 0.5 * rmsnorm(swiglu(rmsnorm(h, g[0]), ffn_w_gu[l, 0], ffn_w_down[l, 0]), g[1])
        xn = rmsnorm(h, g[2])
        if l < N_A_LAYERS:
            mix = short_gated_conv(xn, conv_w_in[l], conv_k[l], conv_w_out[l])
        else:
            j = l - N_A_LAYERS
            lambda_init = 0.8 - 0.6 * math.exp(-0.3 * l)
            mix = diff_attention(xn, k, v, attn_w_q[j], attn_lambda[j],
                                 attn_subln_g[j], attn_w_o[j], lambda_init)
        h = h + rmsnorm(mix, g[3])
        h = h + 0.5 * rmsnorm(swiglu(rmsnorm(h, g[4]), ffn_w_gu[l, 1], ffn_w_down[l, 1]), g[5])
    return h
```

```python
import math
import contextlib
import numpy as np
import ml_dtypes
import concourse.bass as bass
import concourse.mybir as mybir
from concourse.bass_utils import run_bass_kernel_spmd

F32 = mybir.dt.float32
BF16 = mybir.dt.bfloat16
AF = mybir.ActivationFunctionType
ALU = mybir.AluOpType
AX = mybir.AxisListType

D = 1024
DFF = 2816
NJ = 22
SEG = 1024
NSEG = 2
TOK = 2048
S = 4096
NB = 4
EPS = 1e-6
HEAD_GROUPS = [[0, 2, 4, 5], [1, 3, 6, 7]]
LAMBDA_INIT = 0.8 - 0.6 * math.exp(-0.3 * 1)
WSLOT = 3072
NWSLOT = 3
NBIAS = 35
SZ = {F32: 4, BF16: 2}


class R:
    __slots__ = ("ap", "buf", "lo", "hi")

    def __init__(self, ap, buf, lo, hi):
        self.ap, self.buf, self.lo, self.hi = ap, buf, lo, hi


class Buf:
    def __init__(self, name, handle, dtype):
        self.name, self.h, self.dtype = name, handle, dtype
        self.views = {dtype: handle}

    def s(self, lo, n, dtype=None, p0=0, p1=128):
        dt = dtype or self.dtype
        if dt not in self.views:
            self.views[dt] = self.h.bitcast(dt)
        v = self.views[dt]
        sz = SZ[dt]
        return R(v[p0:p1, lo:lo + n], self.name, lo * sz, (lo + n) * sz)


class SubBuf:
    def __init__(self, parent, byte_off, dtype):
        self.parent, self.off, self.dtype = parent, byte_off, dtype
        self.name = parent.name

    def s(self, lo, n, dtype=None, p0=0, p1=128):
        dt = dtype or self.dtype
        return self.parent.s(self.off // SZ[dt] + lo, n, dt, p0, p1)


class Op:
    __slots__ = ("eng", "fn", "deps", "dma_key", "signal", "signaled", "idx", "inc")

    def __init__(self, eng, fn, dma_key, inc=16):
        self.eng, self.fn, self.dma_key, self.inc = eng, fn, dma_key, inc
        self.deps = []
        self.signal = None
        self.signaled = False


ENGS = ["pe", "act", "dve", "pool", "sp"]


class Prog:
    def __init__(self):
        self.nc = bass.Bass("TRN2", target_bir_lowering=False)
        self.ops = []
        self.acc = {}
        self.stack = contextlib.ExitStack()
        self.bufs = {}
        self.dma_count = {}
        self.final_deps = []
        self.last_dma = {}
        self.env = {}
        self.sp_setup = None
        self.ndram = 0
        self.nbank = 0
        self.nss = 0

    def sbuf(self, name, n, dtype):
        h = self.stack.enter_context(self.nc.sbuf_tensor(name, [128, n], dtype))
        b = Buf(name, h, dtype)
        self.bufs[name] = b
        return b

    def psum(self, name, n=512):
        h = self.stack.enter_context(self.nc.psum_tensor(name, [128, n], F32))
        b = Buf(name, h, F32)
        self.bufs[name] = b
        return b

    maxops = None

    def add(self, eng, fn, reads=(), writes=(), dma_key=None, force=False, inc=16):
        if Prog.maxops is not None and len(self.ops) >= Prog.maxops and not force:
            return None
        op = Op(eng, fn, dma_key, inc)
        op.idx = len(self.ops)
        deps = set()
        for r in reads:
            if r is None:
                continue
            a = self.acc.setdefault(r.buf, {"w": [], "r": []})
            for (lo, hi, o) in a["w"]:
                if lo < r.hi and r.lo < hi:
                    deps.add(o)
        for r in writes:
            a = self.acc.setdefault(r.buf, {"w": [], "r": []})
            for (lo, hi, o) in a["w"]:
                if lo < r.hi and r.lo < hi:
                    deps.add(o)
            for (lo, hi, o) in a["r"]:
                if lo < r.hi and r.lo < hi:
                    deps.add(o)
        for r in reads:
            if r is None:
                continue
            a = self.acc[r.buf]
            if eng in ("pe", "act", "dve"):
                a["r"] = [x for x in a["r"] if not (x[0] == r.lo and x[1] == r.hi and x[2].eng == eng
                                                      and x[2].dma_key is None)]
            a["r"].append((r.lo, r.hi, op))
        for r in writes:
            a = self.acc[r.buf]
            a["w"] = [x for x in a["w"] if not (r.lo <= x[0] and x[1] <= r.hi)]
            a["r"] = [x for x in a["r"] if not (r.lo <= x[0] and x[1] <= r.hi)]
            a["w"].append((r.lo, r.hi, op))
        deps.discard(op)
        for d in deps:
            if d.eng == "pe" and eng == "pe" and d.dma_key is None:
                continue
            op.deps.append(d)
            d.signaled = True
        if dma_key is not None:
            op.signaled = True
            prev = self.last_dma.get(dma_key)
            if prev is not None and prev not in op.deps:
                op.deps.append(prev)
            self.last_dma[dma_key] = op
        self.ops.append(op)
        return op

    def mm(self, out, pairs, start=True, stop=True):
        reads = []
        for l, r in pairs:
            reads += [l, r]
        n = len(pairs)

        def fn(e, out=out, pairs=pairs, start=start, stop=stop, n=n):
            ins = None
            for i, (l, r) in enumerate(pairs):
                ins = e.matmul(out.ap, lhsT=l.ap, rhs=r.ap, start=(start and i == 0),
                               stop=(stop and i == n - 1))
            return ins
        return self.add("pe", fn, reads, [out])

    def act(self, out, in_, func, bias=None, scale=None):
        reads = [in_]
        kw = {}
        if bias is not None:
            if isinstance(bias, R):
                reads.append(bias)
                kw["bias"] = bias.ap
            else:
                kw["bias"] = bias
        if scale is not None:
            if isinstance(scale, R):
                reads.append(scale)
                kw["scale"] = scale.ap
            else:
                kw["scale"] = scale

        def fn(e, out=out, in_=in_, func=func, kw=kw):
            return e.activation(out=out.ap, in_=in_.ap, func=func, **kw)
        return self.add("act", fn, reads, [out])

    def tt(self, out, in0, in1, op, eng="dve"):
        def fn(e, out=out, in0=in0, in1=in1, op=op):
            return e.tensor_tensor(out=out.ap, in0=in0.ap, in1=in1.ap, op=op)
        return self.add(eng, fn, [in0, in1], [out])

    def ts(self, out, in0, s1, s2, op0, op1=None, eng="dve"):
        reads = [in0]
        a1 = s1.ap if isinstance(s1, R) else s1
        a2 = s2.ap if isinstance(s2, R) else s2
        if isinstance(s1, R):
            reads.append(s1)
        if isinstance(s2, R):
            reads.append(s2)

        def fn(e, out=out, in0=in0, a1=a1, a2=a2, op0=op0, op1=op1):
            if op1 is None:
                return e.tensor_scalar(out=out.ap, in0=in0.ap, scalar1=a1, scalar2=None, op0=op0)
            return e.tensor_scalar(out=out.ap, in0=in0.ap, scalar1=a1, scalar2=a2, op0=op0, op1=op1)
        return self.add(eng, fn, reads, [out])

    def stt(self, out, in0, scalar, in1, op0, op1, eng="dve"):
        reads = [in0, in1]
        a = scalar.ap if isinstance(scalar, R) else scalar
        if isinstance(scalar, R):
            reads.append(scalar)

        def fn(e, out=out, in0=in0, a=a, in1=in1, op0=op0, op1=op1):
            return e.scalar_tensor_tensor(out=out.ap, in0=in0.ap, scalar=a, in1=in1.ap, op0=op0, op1=op1)
        return self.add(eng, fn, reads, [out])

    def recip(self, out, in_):
        def fn(e, out=out, in_=in_):
            return e.reciprocal(out=out.ap, in_=in_.ap)
        return self.add("dve", fn, [in_], [out])

    def copy(self, out, in_, eng="dve"):
        def fn(e, out=out, in_=in_):
            return e.tensor_copy(out=out.ap, in_=in_.ap)
        return self.add(eng, fn, [in_], [out])

    def rmax(self, out, in_):
        def fn(e, out=out, in_=in_):
            return e.reduce_max(out=out.ap, in_=in_.ap, axis=AX.X)
        return self.add("dve", fn, [in_], [out])

    def memset(self, out, val, eng="dve"):
        def fn(e, out=out, val=val):
            return e.memset(out.ap, val)
        return self.add(eng, fn, [], [out])

    def dma_in(self, eng, out, src_ap, key, dram=None, **kw):
        def fn(e, out=out, src_ap=src_ap, kw=kw):
            src = src_ap(self.env) if callable(src_ap) else src_ap
            return e.dma_start(out=out.ap, in_=src, **kw)
        return self.add(eng, fn, [dram] if dram is not None else [], [out], dma_key=key)

    def dma_out(self, eng, dst_ap, in_, key, final=True, dram=None, **kw):
        def fn(e, dst_ap=dst_ap, in_=in_, kw=kw):
            return e.dma_start(out=dst_ap, in_=in_.ap, **kw)
        op = self.add(eng, fn, [in_], [dram] if dram is not None else [], dma_key=key, force=True)
        if final:
            self.final_deps.append(op)
        return op

    def dreg(self, name, chunk, n=1, whole=False):
        if whole:
            return R(None, "dram_" + name, chunk * 100000, (chunk + n) * 100000)
        self.ndram += 1
        return R(None, "dram_" + name, chunk * 100000 + self.ndram, chunk * 100000 + self.ndram + 1)

    def collective(self, kind, groups, in_ap, out_ap, in_reg, out_reg, key):
        def fn(e, kind=kind, groups=groups, in_ap=in_ap, out_ap=out_ap):
            return e.collective_compute(kind, ALU.bypass, replica_groups=groups, ins=[in_ap], outs=[out_ap])
        return self.add("pool", fn, [in_reg], [out_reg], dma_key=key, inc=1)

    def emit(self):
        nc = self.nc
        st = self.stack
        esem = {e: st.enter_context(nc.semaphore("s_" + e)) for e in ENGS}
        dsem = {}
        cnt = {e: 0 for e in ENGS}
        dcnt = {}
        fin = Op("sp", None, None)
        fin.deps = list(self.final_deps)
        self.ops.append(fin)
        for op in self.ops:
            if op.dma_key is not None:
                if op.dma_key not in dsem:
                    dsem[op.dma_key] = st.enter_context(nc.semaphore("d_" + str(op.dma_key)))
                    dcnt[op.dma_key] = 0
                dcnt[op.dma_key] += op.inc
                op.signal = (dsem[op.dma_key], dcnt[op.dma_key])
            elif op.signaled:
                cnt[op.eng] += 1
                op.signal = (esem[op.eng], cnt[op.eng])
        per = {e: [o for o in self.ops if o.eng == e] for e in ENGS}
        self.stats = {e: len(per[e]) for e in ENGS}
        nwaits = {e: 0 for e in ENGS}

        def run(e, name):
            waited = {}
            for op in per[name]:
                need = {}
                for d in op.deps:
                    sem, val = d.signal
                    k = id(sem)
                    if waited.get(k, 0) >= val:
                        continue
                    if k not in need or need[k][1] < val:
                        need[k] = (sem, val)
                for k, (sem, val) in need.items():
                    e.wait_ge(sem, val)
                    waited[k] = val
                    nwaits[name] += 1
                if op.fn is None:
                    continue
                ins = op.fn(e)
                if op.signal is not None:
                    sem, val = op.signal
                    ins.then_inc(sem, op.inc if op.dma_key is not None else 1)
        with nc.Block() as block:
            @block.tensor
            def _(e):
                run(e, "pe")

            @block.scalar
            def _(e):
                run(e, "act")

            @block.vector
            def _(e):
                run(e, "dve")

            @block.gpsimd
            def _(e):
                run(e, "pool")

            @block.sync
            def _(e):
                if self.sp_setup is not None:
                    self.sp_setup(e, self.env)
                run(e, "sp")
        self.stats["waits"] = nwaits
        st.close()
        return nc


class Tile:
    def __init__(self, K, seg, tt, halo=False):
        self.K, self.seg, self.tt, self.halo = K, seg, tt, halo
        self.n = 2 if halo else 512

    def h(self, c):
        K = self.K
        if self.halo:
            return K.HH.s(c * 2, 2)
        return K.H.s((self.seg * 8 + c) * SEG + self.tt * 512, 512)

    def xn(self, c):
        K = self.K
        if self.halo:
            return K.XNH.s(c * 2, 2)
        return K.XN.s(c * SEG + self.tt * 512, 512)

    def y(self, m):
        K = self.K
        if self.halo:
            return K.YH.s(m * 2, 2)
        return K.XY.s(m * SEG + self.tt * 512, 512)

    def hid(self, j):
        K = self.K
        if self.halo:
            return K.HIDH.s(j * 2, 2)
        return K.HID.s(j * SEG + self.tt * 512, 512)

    def rs(self):
        K = self.K
        if self.halo:
            return K.RSH.s(0, 2)
        return K.RS.s((self.seg * 2 + self.tt) * 512, 512)


class Kern:
    def __init__(self, mode):
        self.mode = mode
        self.P = Prog()
        self.nc = self.P.nc
        self.dram = {}
        self.wcount = 0
        self.tailq = []
        self.sqi = 0
        self.sgi = 0
        self.bki = 0
        self.ssi = 0

    def din(self, name, shape, dtype=F32):
        t = self.nc.dram_tensor(name, list(shape), dtype, kind="ExternalInput")
        self.dram[name] = t
        return t.ap()

    def dout(self, name, shape, dtype=F32):
        t = self.nc.dram_tensor(name, list(shape), dtype, kind="ExternalOutput")
        self.dram[name] = t
        return t.ap()

    nring = 4

    def drain(self, n=None):
        q = self.tailq
        k = len(q) if n is None else min(n, len(q))
        for _ in range(k):
            q.pop(0)()

    def bank(self):
        b = self.PS[self.bki % self.nring]
        self.bki += 1
        return b

    def ssbank(self):
        b = self.PS[4 + self.ssi % 4]
        self.ssi += 1
        return b

    def sq(self, n=512):
        r = self.SQ.s((self.sqi % 4) * 512, n)
        self.sqi += 1
        return r

    def sg(self, n=512):
        r = self.SG.s((self.sgi % 4) * 512, n)
        self.sgi += 1
        return r

    def wload(self, src_ap, nelem):
        slot = self.wcount % NWSLOT
        self.wcount += 1
        dst = self.W.s(slot * WSLOT, nelem)
        self.P.dma_in("pool", dst, src_ap, "w%d" % slot, max_dma_last_dim=8192)
        base = slot * WSLOT
        W = self.W
        return lambda lo, n, p0=0, p1=128: W.s(base + lo, n, None, p0, p1)

    def alloc_common(self):
        P = self.P
        self.H = P.sbuf("H", NSEG * 8 * SEG, F32)
        self.XY = P.sbuf("XY", 8 * SEG, F32)
        self.HID = P.sbuf("HID", NJ * SEG, BF16)
        self.W = P.sbuf("W", NWSLOT * WSLOT, BF16)
        self.SQ = P.sbuf("SQ", 3 * 512, BF16)
        self.SG = P.sbuf("SG", 3 * 512, F32)
        self.RS = P.sbuf("RS", 2 * 512, F32)
        self.CB = P.sbuf("CB", 128 * 4, BF16)
        self.G = P.sbuf("G", 12 * 8 + 8 + 24 + 8, F32)
        self.G5 = P.sbuf("G5", 4 * 8, F32)
        self.PS = [P.psum("ps%d" % i) for i in range(8)]
        self.ONES_D = self.CB.s(0, 128)
        self.ONES_H = self.CB.s(128, 128)
        self.ONES1 = self.CB.s(256, 128)
        self.IDENT = self.CB.s(384, 128)

    def load_consts(self, gains_ap, cb_ap):
        P = self.P
        P.dma_in("sp", self.G.s(0, 12 * 8 + 8 + 24), gains_ap, "cg")
        P.dma_in("sp", self.CB.s(0, 512), cb_ap, "ccb")
        for l in range(2):
            for i, gi in enumerate((1, 5)):
                P.ts(self.G5.s((l * 2 + i) * 8, 8), self.G.s((l * 6 + gi) * 8, 8), 0.5, None, ALU.mult)

    def gcol(self, l, i, c):
        return self.G.s((l * 6 + i) * 8 + c, 1)

    def prenorm(self, tiles, gfn):
        P = self.P
        for t in tiles:
            n = t.n
            bank = self.ssbank()
            for c in range(8):
                sq = self.sq(n)
                P.act(sq, t.h(c), AF.Square)
                P.mm(bank.s(0, n), [(self.ONES_D, sq)], start=(c == 0), stop=(c == 7))
            P.act(t.rs(), bank.s(0, n), AF.Ln, bias=self.EPSC, scale=1.0)
            P.act(t.rs(), t.rs(), AF.Exp, scale=-0.5)
            for c in range(8):
                P.stt(t.xn(c), t.h(c), gfn(c), t.rs(), ALU.mult, ALU.mult)

    def proj_out(self, tiles, src, nk, wsrc, gfn):
        P = self.P
        ssb = [self.ssbank() for _ in tiles]
        pending = None
        for m in range(8):
            w = self.wload(wsrc(m), nk * 128)
            for ti, t in enumerate(tiles):
                n = t.n
                py = self.bank().s(0, n)
                P.mm(py, [(w(k * 128, 128), src(k, t)) for k in range(nk)])
                if pending is not None:
                    P.mm(*pending[0], **pending[1])
                P.copy(t.y(m), py)
                sq = self.sq(n)
                P.act(sq, t.y(m), AF.Square)
                pending = ((ssb[ti].s(0, n), [(self.ONES_D, sq)]), dict(start=(m == 0), stop=(m == 7)))
        P.mm(*pending[0], **pending[1])
        for ti, t in enumerate(tiles):
            n = t.n
            P.act(t.rs(), ssb[ti].s(0, n), AF.Ln, bias=self.EPSC, scale=1.0)
            P.act(t.rs(), t.rs(), AF.Exp, scale=-0.5)
        for m in range(8):
            for ti, t in enumerate(tiles):
                def work(m=m, t=t):
                    tmp = self.sg(t.n)
                    P.stt(tmp, t.y(m), gfn(m), t.rs(), ALU.mult, ALU.mult)
                    P.tt(t.h(m), t.h(m), tmp, ALU.add)
                self.tailq.append(work)

    def ffn(self, tiles, f, wgu_ap, wdn_ap):
        P = self.P
        l, i = f // 2, f % 2
        self.prenorm(tiles, lambda c: self.gcol(l, 0 if i == 0 else 4, c))
        yield
        for j in range(NJ):
            w = self.wload(wgu_ap[f, j], 2048)
            for t in tiles:
                n = t.n
                pg = self.bank().s(0, n)
                pu = self.bank().s(0, n)
                P.mm(pg, [(w(c * 256, 128), t.xn(c)) for c in range(8)])
                P.mm(pu, [(w(c * 256 + 128, 128), t.xn(c)) for c in range(8)])
                sg = self.sg(n)
                P.act(sg, pg, AF.Silu)
                P.tt(t.hid(j), sg, pu, ALU.mult)
            self.drain(1)
        yield
        self.proj_out(tiles, lambda k, t: t.hid(k), NJ, lambda m: wdn_ap[f, m],
                      lambda m: self.G5.s(f * 8 + m, 1))

    def conv_mixer(self, seg, tiles, halo, win_ap, wout_ap):
        P = self.P
        allt = ([halo] if halo is not None else []) + tiles
        self.prenorm(allt, lambda c: self.gcol(0, 2, c))
        yield
        kc = lambda w, m: self.G.s(12 * 8 + 8 + w * 8 + m, 1)
        self.nring = 8
        for m in range(8):
            w = self.wload(win_ap[m], 3072)
            U = lambda lo, n, m=m: self.U.s((m % 2) * 1026 + lo, n)
            if halo is not None:
                pc = self.bank().s(0, 2)
                pz = self.bank().s(0, 2)
                P.mm(pc, [(w(c * 384 + 128, 128), halo.xn(c)) for c in range(8)])
                P.mm(pz, [(w(c * 384 + 256, 128), halo.xn(c)) for c in range(8)])
                cs = self.sg(2)
                P.act(cs, pc, AF.Copy)
                P.tt(U(0, 2), cs, pz, ALU.mult)
            else:
                P.copy(U(0, 2), self.UH.s(m * 2, 2))
            for t in tiles:
                tt = t.tt
                pb = self.bank().s(0, 512)
                pc = self.bank().s(0, 512)
                pz = self.bank().s(0, 512)
                P.mm(pb, [(w(c * 384, 128), t.xn(c)) for c in range(8)])
                P.mm(pc, [(w(c * 384 + 128, 128), t.xn(c)) for c in range(8)])
                P.mm(pz, [(w(c * 384 + 256, 128), t.xn(c)) for c in range(8)])
                cs = self.sg()
                P.act(cs, pc, AF.Copy)
                P.tt(U(2 + tt * 512, 512), cs, pz, ALU.mult)
                t1 = self.sg()
                P.act(t1, U(tt * 512 + 2, 512), AF.Copy, scale=kc(2, m))
                P.stt(t1, U(tt * 512 + 1, 512), kc(1, m), t1, ALU.mult, ALU.add)
                P.stt(t1, U(tt * 512, 512), kc(0, m), t1, ALU.mult, ALU.add)
                P.tt(t.hid(m), t1, pb, ALU.mult)
            if seg == 0:
                P.copy(self.UH.s(m * 2, 2), U(1024, 2))
            self.drain(3)
        self.nring = 4
        yield
        self.proj_out(tiles, lambda k, t: t.hid(k), 8, lambda m: wout_ap[m],
                      lambda m: self.gcol(0, 3, m))

    def qk_proj(self, tiles, seg, w_ap, dst_ap, after_hl=None):
        P = self.P
        for hi in range(8):
            hl, r = hi // 2, hi % 2
            hd = HEAD_GROUPS[r][hl]
            w = self.wload(w_ap[hd], 1024)
            for t in tiles:
                pk = self.bank().s(0, 512)
                P.mm(pk, [(w(c * 128, 128), t.xn(c)) for c in range(8)])
                slot = self.ksi % 3
                self.ksi += 1
                ksb = self.KSB.s(slot * 512, 512)
                P.act(ksb, pk, AF.Copy)
                c0 = seg * SEG + t.tt * 512
                P.dma_out("sp", dst_ap(r, hl)[:, c0:c0 + 512], ksb, "ks%d" % slot, final=False,
                          dram=P.dreg("xs", hl * 2 + r))
            self.drain(3)
            if after_hl is not None and r == 1:
                after_hl(hl)


PAIRS = [[0, 1], [2, 3], [4, 5], [6, 7]]


def build_F():
    K = Kern("F")
    P = K.P
    nc = K.nc
    xT = K.din("xT", [128, NSEG * 8 * SEG])
    xh = K.din("xh", [128, 16])
    gains = K.din("gains", [128, 128])
    cb = K.din("cb", [128, 512], BF16)
    cb2 = K.din("cb2", [128, 384], BF16)
    oh_in = K.din("oh", [128, 2])
    wgu = K.din("wgu", [4, NJ, 128, 2048])
    wdn = K.din("wdn", [4, 8, 128, DFF])
    win = K.din("win", [8, 128, 3072])
    wout = K.din("wout", [8, 128, 1024])
    wk = K.din("wk", [8, 128, 1024])
    wv = K.din("wv", [4, 128, 2048])
    wq = K.din("wq", [8, 128, 1024])
    wo = K.din("wo", [8, 128, 1024])
    alibi = K.din("alibi", [128, 4 * NBIAS])
    tq = K.din("tq", [4, S], BF16)
    lamp = K.din("lamp", [128, 256])
    subg = K.din("subg", [128, 1])
    rank_in = K.din("rank", [1, 1], mybir.dt.int32)
    out = K.dout("out", [128, NSEG * 8 * SEG])
    xs = nc.dram_tensor("xs", [2 * 3 * 4 * 128, TOK], BF16)
    xg = nc.dram_tensor("xg", [2 * 2 * 3 * 4 * 128, TOK], BF16)
    os_ = nc.dram_tensor("os", [2 * 4 * 128, TOK], BF16)
    og = nc.dram_tensor("og", [2 * 2 * 4 * 128, TOK], BF16)
    xs5 = xs.ap().rearrange("(h r k p) t -> h r k p t", h=4, r=2, k=3)
    kt_o = lambda r, hl: xs5[hl, r, 0]
    qt_o = lambda r, hl: xs5[hl, r, 1]
    v_o = lambda r: xs5[:, r, 2].rearrange("h p (b e) -> h p b e", b=16)
    xg6 = xg.ap().rearrange("(h r s k p) t -> h r s k p t", h=4, r=2, s=2, k=3)
    os4 = os_.ap().rearrange("(h d p) t -> h d p t", h=4, d=2)
    og5 = og.ap().rearrange("(h s d p) t -> h s d p t", h=4, s=2, d=2)

    def sp_setup(e, env):
        reg = e.alloc_register("rank")
        e.reg_load(reg, rank_in[0:1, 0:1])
        env["rank"] = e.snap(reg, min_val=0, max_val=1)
    P.sp_setup = sp_setup

    XH = P.sbuf("XH", 19456, F32)
    K.H = P.sbuf("H", NSEG * 8 * SEG, F32)
    K.XY = SubBuf(XH, 0, F32)
    AUX = P.sbuf("AUX", 13832, BF16)
    K.XN = SubBuf(AUX, 0, BF16)
    K.HID = SubBuf(XH, 32768, BF16)
    K.W = P.sbuf("W", NWSLOT * WSLOT, BF16)
    K.SQ = P.sbuf("SQ", 4 * 512, BF16)
    K.SG = P.sbuf("SG", 4 * 512, F32)
    K.RS = P.sbuf("RS", 4 * 512, F32)
    K.CB = P.sbuf("CB", 512, BF16)
    K.G = P.sbuf("G", 136, F32)
    K.G5 = P.sbuf("G5", 32, F32)
    PP = [P.psum("pp%d" % i, 1024) for i in range(4)]
    K.PS = [SubBuf(PP[i // 2], (i % 2) * 2048, F32) for i in range(8)]
    K.ONES_D = K.CB.s(0, 128)
    K.ONES_H = K.CB.s(128, 128)
    K.ONES1 = K.CB.s(256, 128)
    K.IDENT = K.CB.s(384, 128)
    K.U = SubBuf(AUX, 16384, F32)
    K.UH = P.sbuf("UH", 16, F32)
    K.HH = P.sbuf("HH", 16, F32)
    K.XNH = P.sbuf("XNH", 16, BF16)
    K.YH = P.sbuf("YH", 16, F32)
    K.HIDH = P.sbuf("HIDH", NJ * 2, BF16)
    K.RSH = P.sbuf("RSH", 2, F32)
    K.KSB = SubBuf(AUX, 24592, BF16)
    K.EPSB = P.sbuf("EPSB", 1, F32)
    K.ksi = 0
    RAW = [SubBuf(XH, 0, BF16), SubBuf(AUX, 0, BF16)]
    KZ = [SubBuf(XH, 24576, BF16), SubBuf(XH, 32768, BF16)]
    QZ = [SubBuf(XH, 40960, BF16), SubBuf(XH, 49152, BF16)]
    PR = SubBuf(XH, 57344, BF16)
    AT = SubBuf(XH, 63488, F32)
    OB = SubBuf(XH, 71680, BF16)
    CB2 = P.sbuf("CB2", 384, BF16)
    OH = P.sbuf("OH", 8, F32)
    AL = SubBuf(K.RS, 0, F32)
    BI = SubBuf(K.RS, 576, F32)
    LM = SubBuf(K.RS, 1152, F32)
    SUBG = SubBuf(K.RS, 2752, F32)
    MX = SubBuf(K.RS, 2816, F32)
    MASKNEG = CB2.s(0, 128)
    E = [CB2.s(128, 128), CB2.s(256, 128)]
    PS = K.PS
    SQ = K.SQ
    ONES_H, ONES1, IDENT = K.ONES_H, K.ONES1, K.IDENT

    P.memset(K.EPSB.s(0, 1), EPS)
    K.EPSC = K.EPSB.s(0, 1)
    K.load_consts(gains[:, 0:128], cb)
    xops = []
    x4 = xT.rearrange("p (s c t) -> p s c t", s=NSEG, c=8)
    for seg in range(NSEG):
        for tt in range(2):
            if seg == 1 and tt == 0:
                P.dma_in("sp", K.HH.s(0, 16), xh, "xh")
            dstr = K.H.s(seg * 8 * SEG, 8 * SEG)
            dst3 = R(dstr.ap.rearrange("p (c t) -> p c t", c=8)[:, :, tt * 512:(tt + 1) * 512], dstr.buf,
                     dstr.lo, dstr.hi)
            srcx = x4[:, seg, :, tt * 512:(tt + 1) * 512]

            def fn(e, dst3=dst3, srcx=srcx):
                return e.dma_start(out=dst3.ap, in_=srcx)
            op = P.add("sp", fn, [], [K.H.s((seg * 8 + c) * SEG + tt * 512, 512) for c in range(8)],
                       dma_key="x%d" % (seg * 2 + tt))
            if seg == 1:
                op.deps += xops
            else:
                xops.append(op)
    P.dma_in("sp", CB2.s(0, 384), cb2, "c1")
    P.dma_in("sp", OH.s(0, 2), oh_in, "c6")

    TL = [[Tile(K, seg, 0), Tile(K, seg, 1)] for seg in range(NSEG)]
    HALO = [Tile(K, 0, 0, halo=True), None]

    def st_ffn0(seg):
        return K.ffn(([HALO[seg]] if HALO[seg] else []) + TL[seg], 0, wgu, wdn)

    def st_conv(seg):
        return K.conv_mixer(seg, TL[seg], HALO[seg], win, wout)

    def st_ffn1(seg):
        return K.ffn(TL[seg], 1, wgu, wdn)

    def st_kv(seg):
        tiles = TL[seg]
        K.prenorm(tiles, lambda c: K.G.s(12 * 8 + c, 1))
        yield
        K.qk_proj(tiles, seg, wk, kt_o)
        for n4 in range(4):
            w = K.wload(wv[n4], 2048)
            for tb in range(8):
                t = tiles[tb // 4]
                pv = K.bank().s(0, 256)
                P.mm(pv, [(K.XN.s(c * SEG + t.tt * 512 + (tb % 4) * 128, 128), w(c * 256, 256))
                          for c in range(8)])
                P.act(K.HID.s(tb * 1024 + n4 * 256, 256), pv, AF.Copy)
        for tb in range(8):
            for r in range(2):
                src = K.HID.s(tb * 1024 + r * 512, 512)
                dst = v_o(r)[:, :, seg * 8 + tb, :].rearrange("h p e -> p h e")
                src3 = R(src.ap.rearrange("p (h e) -> p h e", h=4), src.buf, src.lo, src.hi)
                op = P.dma_out("sp", dst, src3, "vs%d" % (tb * 2 + r), final=False, dram=P.dreg("xs", r))
                for hl in range(1, 4):
                    P.acc.setdefault("dram_xs", {"w": [], "r": []})["w"].append(
                        (P.dreg("xs", hl * 2 + r).lo, P.dreg("xs", hl * 2 + r).hi, op))
        yield

    def st_ffn2(seg):
        return K.ffn(TL[seg], 2, wgu, wdn)

    def st_q(seg):
        K.prenorm(TL[seg], lambda c: K.gcol(1, 2, c))
        yield
        def exch1(hl):
            for ci in (hl * 2, hl * 2 + 1):
                P.collective("AllGather", PAIRS, xs.ap()[ci * 384:(ci + 1) * 384, :].opt(),
                             xg.ap()[ci * 768:(ci + 1) * 768, :].opt(), P.dreg("xs", ci, whole=True),
                             P.dreg("xg", ci, whole=True), "ccx%d" % ci)

        def exch(hl):
            if hl >= 1:
                exch1(hl - 1)
            if hl == 3:
                exch1(3)
        K.qk_proj(TL[seg], seg, wq, qt_o, after_hl=(exch if seg == NSEG - 1 else None))
        yield

    def run_pipeline(stages):
        gens = [st(seg) for st in stages for seg in range(NSEG)]
        next(gens[0])
        for i, g in enumerate(gens):
            next(g)
            K.drain()
            if i + 1 < len(gens):
                next(gens[i + 1])
            for _ in g:
                pass
        K.drain()

    run_pipeline((st_ffn0, st_conv, st_ffn1, st_kv, st_ffn2, st_q))

    P.dma_in("sp", AL.s(0, 4 * NBIAS), alibi, "c3")
    P.dma_in("sp", LM.s(0, 256), lamp, "c4")
    P.dma_in("sp", SUBG.s(0, 1), subg, "c5")
    P.memset(KZ[0].s(0, S, None, 64, 128), 0.0)
    P.memset(KZ[0].s(0, S, None, 64, 65), 1.0)
    P.memset(KZ[1].s(0, S, None, 0, 64), 0.0)
    P.memset(KZ[1].s(0, S, None, 0, 1), 1.0)
    P.memset(QZ[0].s(0, S, None, 64, 128), 0.0)
    P.memset(QZ[1].s(0, S, None, 0, 64), 0.0)
    for i in range(2):
        P.tt(LM.s(256 + i * 64, 64), LM.s(i * 128, 64), LM.s(i * 128 + 64, 64), ALU.mult)

        def fn(e, i=i):
            return e.reduce_sum(out=LM.s(384 + i, 1).ap, in_=LM.s(256 + i * 64, 64).ap, axis=AX.X)
        P.add("dve", fn, [LM.s(256 + i * 64, 64)], [LM.s(384 + i, 1)])
        P.act(LM.s(386 + i, 1), LM.s(384 + i, 1), AF.Exp)
    P.tt(LM.s(388, 1), LM.s(387, 1), LM.s(386, 1), ALU.subtract)
    P.ts(LM.s(389, 1), LM.s(388, 1), -LAMBDA_INIT, None, ALU.add)
    NEGLAM = LM.s(389, 1)
    P.ts(SUBG.s(1, 1), SUBG.s(0, 1), 1.0 - LAMBDA_INIT, None, ALU.mult)
    SUBG2 = SUBG.s(1, 1)

    sqi = [0]
    sbi = [0]
    MSQ = SubBuf(XH, 73728, BF16)

    def head_load(hl):
        def srcfn(env, hl=hl):
            return xg6[hl, bass.ds(env["rank"], 1)].rearrange("o s k p t -> (o p) (s k) t")
        dstr = RAW[hl % 2].s(0, 3 * S)
        dst4 = R(dstr.ap.rearrange("p (sk t) -> p sk t", sk=6), dstr.buf, dstr.lo, dstr.hi)
        P.dma_in("sp", dst4, srcfn, "kv%d" % (hl % 2), dram=P.dreg("xg", hl * 2, 2, whole=True))

    EBD = E[0]
    head_load(0)
    for hl in range(4):
        slot = 0
        KV = RAW[hl % 2]
        if hl + 1 < 4:
            head_load(hl + 1)
        P.dma_in("sp", QZ[0].s(0, S, None, 64, 65), tq[hl:hl + 1, :], "tq0")
        P.dma_in("sp", QZ[1].s(0, S, None, 0, 1), tq[hl:hl + 1, :], "tq1")

        def kcol(kb):
            return (kb // 16) * 3 * TOK + (kb % 16) * 128

        def qcol(g):
            return (g // 4) * 3 * TOK + TOK + (g % 4) * 512

        def vcol(kb):
            return (kb // 16) * 3 * TOK + 2 * TOK + (kb % 16) * 128

        def bound_ops(h2):
            RW = RAW[h2 % 2]
            mo = (h2 % 2) * 24
            ops = []
            pend = []

            def op_all():
                tiles16 = [(xi, g) for xi in range(2) for g in range(8)]
                for t in range(18):
                    if t < 16:
                        xi, g = tiles16[t]
                        sq = MSQ.s((sqi[0] % 4) * 512, 512)
                        sqi[0] += 1
                        srcr = RW.s(qcol(g) if xi == 0 else kcol(4 * g), 512)
                        P.tt(sq, srcr, srcr, ALU.mult)
                        pend.append((xi, g, sq))
                    if t >= 2:
                        xi2, g2, sq2 = pend.pop(0)
                        bk = PS[sbi[0] % 4].s(0, 512)
                        sbi[0] += 1
                        P.mm(bk, [(EBD, sq2)])
                        P.rmax(MX.s(mo + xi2 * 8 + g2, 1), bk)
            ops.append(op_all)

            def fin():
                for xi in range(2):
                    P.rmax(MX.s(mo + 16 + xi, 1), MX.s(mo + xi * 8, 8))
                P.tt(MX.s(mo + 18, 1), MX.s(mo + 16, 1), MX.s(mo + 17, 1), ALU.mult)
                P.act(MX.s(mo + 19, 1), MX.s(mo + 18, 1), AF.Ln, scale=1.0 / 64.0)
                P.act(MX.s(mo + 19, 1), MX.s(mo + 19, 1), AF.Exp, scale=0.5)
                msel = MSQ.s((sqi[0] % 4) * 512, 2)
                sqi[0] += 1
                P.ts(msel, OH.s(0, 2), MX.s(mo + 19, 1), None, ALU.mult)
                bk = PS[sbi[0] % 4].s(0, 2)
                sbi[0] += 1
                P.mm(bk, [(ONES1, msel)])
                P.copy(MX.s(mo + 20, 2), bk)
                P.tt(MX.s(mo + 22, 1), MX.s(mo + 20, 1), MX.s(mo + 21, 1), ALU.max)
                P.ts(BI.s((h2 % 2) * NBIAS, NBIAS), AL.s(h2 * NBIAS, NBIAS), MX.s(mo + 22, 1), None, ALU.subtract)
            ops.append(fin)
            return ops

        for op in bound_ops(hl):
            op()
        slot = hl % 2
        mq = []

        for g in range(8):
            for c in range(2):
                p0, p1 = c * 64, c * 64 + 64
                P.copy(KZ[c].s(g * 512, 512, None, p0, p1), KV.s(kcol(4 * g), 512, None, p0, p1))
                P.act(QZ[c].s(g * 512, 512, None, p0, p1), KV.s(qcol(g), 512, None, p0, p1), AF.Copy)

        steps = [(g, kb) for g in range(8) for kb in range(4 * g + 4)]
        Ob = [PS[4], PS[5]]
        Lb = [PS[6], PS[7]]

        def emit_score(i):
            g, kb = steps[i]
            j = kb - 4 * g
            cs = max(j, 0) * 128
            ncol = 512 - cs
            sp = PP[i % 2]
            ops = []
            reads = []
            for c in range(2):
                kop = KZ[c].s(kb * 128, 128)
                qop = QZ[c].s(g * 512 + cs, ncol)
                reads += [kop, qop]
                ops.append((sp.s(c * 512 + cs, ncol), kop, qop, sp.s(c * 512 + cs, 128)))
            if j >= 0:
                reads += [IDENT, MASKNEG]

            def fn(e, ops=ops, j=j):
                ins = None
                for so, kop, qop, sm in ops:
                    ins = e.matmul(so.ap, lhsT=kop.ap, rhs=qop.ap, start=True, stop=(j < 0))
                    if j >= 0:
                        ins = e.matmul(sm.ap, lhsT=IDENT.ap, rhs=MASKNEG.ap, start=False, stop=True)
                return ins
            P.add("pe", fn, reads, [sp.s(0, 1024)])

        emit_score(0)
        deferred = []
        for i, (g, kb) in enumerate(steps):
            while deferred and deferred[0][0] <= i:
                deferred.pop(0)[1](i)
            nkb = 4 * g + 4
            j = kb - 4 * g
            cs = max(j, 0) * 128
            ncol = 512 - cs
            last = (kb == nkb - 1)
            spr = PP[i % 2].s(0, 1024)
            prr = PR.s((i % 3) * 1024, 1024)
            in3 = R(spr.ap.rearrange("p (c n) -> p c n", c=2)[:, :, cs:512], spr.buf, spr.lo, spr.hi)
            out3 = R(prr.ap.rearrange("p (c n) -> p c n", c=2)[:, :, cs:512], prr.buf, prr.lo, prr.hi)
            bias = BI.s(slot * NBIAS + (4 * g - kb + 3), 1)
            P.act(out3, in3, AF.Exp, bias=bias, scale=0.125)
            if i + 1 < len(steps):
                emit_score(i + 1)
            vop = KV.s(vcol(kb), 128)
            for c in range(2):
                pr = PR.s((i % 3) * 1024 + c * 512 + cs, ncol)
                P.mm(Ob[c].s(cs, ncol), [(vop, pr)], start=(kb == 0), stop=last)
                P.mm(Lb[c].s(cs, ncol), [(ONES1, pr)], start=(kb == 0), stop=last)
            if not last:
                continue
            AT4 = [(AT if g % 2 == 0 else K.SG).s(k * 512, 512) for k in range(4)]
            a0, a1, a2, a3 = AT4
            P.copy(a0, Lb[0].s(0, 512))
            P.act(a1, Ob[0].s(0, 512), AF.Copy)
            P.copy(a2, Lb[1].s(0, 512))
            P.act(a3, Ob[1].s(0, 512), AF.Copy)
            P.recip(a0, a0)
            P.tt(a1, a1, a0, ALU.mult)
            P.recip(a2, a2)
            P.tt(a3, a3, a2, ALU.mult)
            P.stt(a1, a3, NEGLAM, a1, ALU.mult, ALU.add)
            sq = MSQ.s((sqi[0] % 4) * 512, 512)
            sqi[0] += 1
            P.tt(sq, a1, a1, ALU.mult)

            def tail(icur, g=g, a0=a0, a1=a1, sq=sq):
                bk = PP[(icur + 1) % 2].s(0, 512)
                P.mm(bk, [(ONES_H, sq)])
                P.act(a0, bk, AF.Ln, bias=K.EPSC, scale=1.0)
                P.act(a0, a0, AF.Exp, scale=-0.5)
                ob = OB.s((g % 2) * 512, 512)
                P.stt(ob, a1, SUBG2, a0, ALU.mult, ALU.mult)
                P.dma_out("sp", os4[hl, g // 4, :, (g % 4) * 512:(g % 4) * 512 + 512], ob, "ob%d" % (g % 2),
                          final=False, dram=P.dreg("os", hl))
            deferred.append((i + 12, tail))
        while deferred:
            deferred.pop(0)[1](len(steps))
        while mq:
            mq.pop(0)()
        P.collective("AllGather", PAIRS, os_.ap()[hl * 256:(hl + 1) * 256, :].opt(),
                     og.ap()[hl * 512:(hl + 1) * 512, :].opt(), P.dreg("os", hl, whole=True),
                     P.dreg("og", hl, whole=True), "cco%d" % hl)

    ogreg = P.dreg("og", 0, 4, whole=True)

    def st_wo(seg):
        def srcfn(env, seg=seg):
            return og5[:, :, bass.ds(env["rank"], 1), :, seg * SEG:(seg + 1) * SEG].rearrange(
                "h s o p t -> (o p) (h s) t")
        dstr = K.HID.s(0, 8 * SEG)
        dst3 = R(dstr.ap.rearrange("p (c t) -> p c t", c=8), dstr.buf, dstr.lo, dstr.hi)
        yield
        P.dma_in("sp", dst3, srcfn, "at", dram=ogreg)
        yield
        K.proj_out(TL[seg], lambda k, t: t.hid(k), 8, lambda m: wo[m], lambda m: K.gcol(1, 3, m))

    def st_ffn3(seg):
        return K.ffn(TL[seg], 3, wgu, wdn)

    run_pipeline((st_wo, st_ffn3))
    for q in range(4):
        n = NSEG * 8 * SEG // 4
        P.dma_out("sp", out[:, q * n:(q + 1) * n], K.H.s(q * n, n), "ho%d" % q)
    nc = P.emit()
    return nc, P


def _tile_cols(w, cols):
    kk = w.shape[0] // 128
    t = w[:, cols].reshape(kk, 128, len(cols)).transpose(1, 0, 2)
    return np.ascontiguousarray(t).reshape(128, -1)


def _prep_weights(ffn_w_gu, ffn_w_down, conv_w_in, conv_w_out, w_kv, attn_w_q, attn_w_o):
    ar = np.arange
    wgu = np.empty((4, NJ, 128, 2048), np.float32)
    wdn = np.empty((4, 8, 128, DFF), np.float32)
    for f in range(4):
        l, i = f // 2, f % 2
        g = ffn_w_gu[l, i]
        for j in range(NJ):
            cols = np.concatenate([ar(j * 128, j * 128 + 128), DFF + ar(j * 128, j * 128 + 128)])
            wgu[f, j] = _tile_cols(g, cols)
        dn = ffn_w_down[l, i]
        for m in range(8):
            wdn[f, m] = _tile_cols(dn, ar(m * 128, m * 128 + 128))
    win = np.empty((8, 128, 3072), np.float32)
    for m in range(8):
        cols = np.concatenate([ar(m * 128, m * 128 + 128), D + ar(m * 128, m * 128 + 128),
                               2 * D + ar(m * 128, m * 128 + 128)])
        win[m] = _tile_cols(conv_w_in[0], cols)
    sq = lambda w: np.stack([_tile_cols(w, ar(m * 128, m * 128 + 128)) for m in range(8)])
    wout = sq(conv_w_out[0])
    wk = sq(w_kv[:, :D])
    wq = sq(attn_w_q[0])
    perm = np.concatenate([D + ar(h * 128, h * 128 + 128) for g in HEAD_GROUPS for h in g])
    perm_o = np.concatenate([ar(HEAD_GROUPS[r][hl] * 128, HEAD_GROUPS[r][hl] * 128 + 128)
                             for hl in range(4) for r in range(2)])
    wo = sq(attn_w_o[0][perm_o])
    wv = np.stack([_tile_cols(w_kv, perm[n * 256:(n + 1) * 256]) for n in range(4)])
    return dict(wgu=wgu, wdn=wdn, win=win, wout=wout, wk=wk, wq=wq, wo=wo, wv=wv)


def _const_cb():
    cb = np.zeros((128, 512), np.float32)
    cb[:, 0:128] = 1.0 / D
    cb[:, 128:256] = 1.0 / 128.0
    cb[:, 256:384] = 1.0
    cb[:, 384:512] = np.eye(128, dtype=np.float32)
    return cb.astype(ml_dtypes.bfloat16)


def _const_cb2():
    cb2 = np.zeros((128, 384), np.float32)
    k = np.arange(128)[:, None]
    q = np.arange(128)[None, :]
    cb2[:, 0:128] = np.where(k > q, -30000.0, 0.0)
    cb2[0:64, 128:192] = 1.0
    cb2[64:128, 192:256] = 1.0
    cb2[64:128, 256:384] = 1.0
    return cb2.astype(ml_dtypes.bfloat16)


_CACHE = {}


def _get(name, fn):
    if name not in _CACHE:
        _CACHE[name] = fn()
    return _CACHE[name]


def make_maps(x, ffn_w_gu, ffn_w_down, norm_g, conv_w_in, conv_k, conv_w_out, kv_norm_g, w_kv,
              attn_w_q, attn_lambda, attn_subln_g, attn_w_o, cores=range(8)):
    f32 = np.float32
    x = np.asarray(x, f32)
    Wt = _prep_weights(np.asarray(ffn_w_gu, f32), np.asarray(ffn_w_down, f32), np.asarray(conv_w_in, f32),
                       np.asarray(conv_w_out, f32), np.asarray(w_kv, f32), np.asarray(attn_w_q, f32),
                       np.asarray(attn_w_o, f32))
    gains = np.zeros((128, 128), f32)
    gains[:, 0:96] = np.asarray(norm_g, f32).reshape(2, 6, 8, 128).transpose(3, 0, 1, 2).reshape(128, 96)
    gains[:, 96:104] = np.asarray(kv_norm_g, f32).reshape(8, 128).T
    gains[:, 104:128] = np.asarray(conv_k, f32)[0].reshape(3, 8, 128).transpose(2, 0, 1).reshape(128, 24)
    cb = _const_cb()
    cb2 = _const_cb2()
    slopes = 2.0 ** (-8.0 * np.arange(1, 9) / 8.0)
    lam_in = np.tile(np.asarray(attn_lambda, f32)[0].reshape(1, 256), (128, 1))
    subg = np.asarray(attn_subln_g, f32)[0].reshape(128, 1)
    ki = np.arange(128, dtype=np.float64)[:, None]
    npr = np.arange(NBIAS, dtype=np.float64)[None, :] - 3.0
    oh = np.zeros((128, 2), f32)
    oh[0, 0] = 1.0
    oh[64, 1] = 1.0
    maps = []
    for c in cores:
        b, half = c // 2, c % 2
        xs = x[b, half * TOK:(half + 1) * TOK]
        xT = xs.reshape(NSEG, SEG, 8, 128).transpose(3, 0, 2, 1).reshape(128, NSEG * 8 * SEG)
        if half == 0:
            xh = np.zeros((128, 16), f32)
        else:
            xh = x[b, TOK - 2:TOK].reshape(2, 8, 128).transpose(2, 1, 0).reshape(128, 16)
        al = np.zeros((128, 4, NBIAS), f32)
        tq = np.zeros((4, S), f32)
        for hl, hd in enumerate(HEAD_GROUPS[half]):
            al[:, hl, :] = slopes[hd] * (ki - 128.0 * npr)
            tq[hl, :] = -8.0 * slopes[hd] * (np.arange(S) % 512)
        maps.append(dict(xT=np.ascontiguousarray(xT), xh=np.ascontiguousarray(xh), gains=gains, cb=cb, cb2=cb2,
                         wgu=Wt["wgu"], wdn=Wt["wdn"], win=Wt["win"], wout=Wt["wout"], wk=Wt["wk"],
                         wv=Wt["wv"], wq=Wt["wq"], wo=Wt["wo"], alibi=al.reshape(128, 4 * NBIAS),
                         tq=tq.astype(ml_dtypes.bfloat16), lamp=lam_in, subg=subg,
                         rank=np.array([[half]], np.int32), oh=oh))
    return maps


def kernel(x, ffn_w_gu, ffn_w_down, norm_g, conv_w_in, conv_k, conv_w_out, kv_norm_g, w_kv,
           attn_w_q, attn_lambda, attn_subln_g, attn_w_o):
    maps = make_maps(x, ffn_w_gu, ffn_w_down, norm_g, conv_w_in, conv_k, conv_w_out, kv_norm_g, w_kv,
                     attn_w_q, attn_lambda, attn_subln_g, attn_w_o)
    nc, _ = _get("F", build_F)
    res = run_bass_kernel_spmd(nc, maps, core_ids=list(range(8))).results
    out = np.empty((NB, S, D), np.float32)
    for c in range(8):
        b, half = c // 2, c % 2
        o = res[c]["out"].reshape(128, NSEG, 8, SEG).transpose(1, 3, 2, 0).reshape(TOK, D)
        out[b, half * TOK:(half + 1) * TOK] = o
    return out
```

```python
import math
import contextlib
import numpy as np
import ml_dtypes
import concourse.bass as bass
import concourse.mybir as mybir
from concourse.bass_utils import run_bass_kernel_spmd

F32 = mybir.dt.float32
BF16 = mybir.dt.bfloat16
AF = mybir.ActivationFunctionType
ALU = mybir.AluOpType
AX = mybir.AxisListType

D = 1024
DFF = 2816
NJ = 22
SEG = 1024
NSEG = 2
TOK = 2048
S = 4096
NB = 4
EPS = 1e-6
HEAD_GROUPS = [[0, 2, 4, 5], [1, 3, 6, 7]]
LAMBDA_INIT = 0.8 - 0.6 * math.exp(-0.3 * 1)
WSLOT = 3072
NWSLOT = 3
NBIAS = 35
SZ = {F32: 4, BF16: 2}


class R:
    __slots__ = ("ap", "buf", "lo", "hi")

    def __init__(self, ap, buf, lo, hi):
        self.ap, self.buf, self.lo, self.hi = ap, buf, lo, hi


class Buf:
    def __init__(self, name, handle, dtype):
        self.name, self.h, self.dtype = name, handle, dtype
        self.views = {dtype: handle}

    def s(self, lo, n, dtype=None, p0=0, p1=128):
        dt = dtype or self.dtype
        if dt not in self.views:
            self.views[dt] = self.h.bitcast(dt)
        v = self.views[dt]
        sz = SZ[dt]
        return R(v[p0:p1, lo:lo + n], self.name, lo * sz, (lo + n) * sz)


class SubBuf:
    def __init__(self, parent, byte_off, dtype):
        self.parent, self.off, self.dtype = parent, byte_off, dtype
        self.name = parent.name

    def s(self, lo, n, dtype=None, p0=0, p1=128):
        dt = dtype or self.dtype
        return self.parent.s(self.off // SZ[dt] + lo, n, dt, p0, p1)


class Op:
    __slots__ = ("eng", "fn", "deps", "dma_key", "signal", "signaled", "idx", "inc")

    def __init__(self, eng, fn, dma_key, inc=16):
        self.eng, self.fn, self.dma_key, self.inc = eng, fn, dma_key, inc
        self.deps = []
        self.signal = None
        self.signaled = False


ENGS = ["pe", "act", "dve", "pool", "sp"]


class Prog:
    def __init__(self):
        self.nc = bass.Bass("TRN2", target_bir_lowering=False)
        self.ops = []
        self.acc = {}
        self.stack = contextlib.ExitStack()
        self.bufs = {}
        self.dma_count = {}
        self.final_deps = []
        self.last_dma = {}
        self.env = {}
        self.sp_setup = None
        self.ndram = 0
        self.nbank = 0
        self.nss = 0

    def sbuf(self, name, n, dtype):
        h = self.stack.enter_context(self.nc.sbuf_tensor(name, [128, n], dtype))
        b = Buf(name, h, dtype)
        self.bufs[name] = b
        return b

    def psum(self, name, n=512):
        h = self.stack.enter_context(self.nc.psum_tensor(name, [128, n], F32))
        b = Buf(name, h, F32)
        self.bufs[name] = b
        return b

    maxops = None

    def add(self, eng, fn, reads=(), writes=(), dma_key=None, force=False, inc=16):
        if Prog.maxops is not None and len(self.ops) >= Prog.maxops and not force:
            return None
        op = Op(eng, fn, dma_key, inc)
        op.idx = len(self.ops)
        deps = set()
        for r in reads:
            if r is None:
                continue
            a = self.acc.setdefault(r.buf, {"w": [], "r": []})
            for (lo, hi, o) in a["w"]:
                if lo < r.hi and r.lo < hi:
                    deps.add(o)
        for r in writes:
            a = self.acc.setdefault(r.buf, {"w": [], "r": []})
            for (lo, hi, o) in a["w"]:
                if lo < r.hi and r.lo < hi:
                    deps.add(o)
            for (lo, hi, o) in a["r"]:
                if lo < r.hi and r.lo < hi:
                    deps.add(o)
        for r in reads:
            if r is None:
                continue
            a = self.acc[r.buf]
            if eng in ("pe", "act", "dve"):
                a["r"] = [x for x in a["r"] if not (x[0] == r.lo and x[1] == r.hi and x[2].eng == eng
                                                      and x[2].dma_key is None)]
            a["r"].append((r.lo, r.hi, op))
        for r in writes:
            a = self.acc[r.buf]
            a["w"] = [x for x in a["w"] if not (r.lo <= x[0] and x[1] <= r.hi)]
            a["r"] = [x for x in a["r"] if not (r.lo <= x[0] and x[1] <= r.hi)]
            a["w"].append((r.lo, r.hi, op))
        deps.discard(op)
        for d in deps:
            if d.eng == "pe" and eng == "pe" and d.dma_key is None:
                continue
            op.deps.append(d)
            d.signaled = True
        if dma_key is not None:
            op.signaled = True
            prev = self.last_dma.get(dma_key)
            if prev is not None and prev not in op.deps:
                op.deps.append(prev)
            self.last_dma[dma_key] = op
        self.ops.append(op)
        return op

    def mm(self, out, pairs, start=True, stop=True):
        reads = []
        for l, r in pairs:
            reads += [l, r]
        n = len(pairs)

        def fn(e, out=out, pairs=pairs, start=start, stop=stop, n=n):
            ins = None
            for i, (l, r) in enumerate(pairs):
                ins = e.matmul(out.ap, lhsT=l.ap, rhs=r.ap, start=(start and i == 0),
                               stop=(stop and i == n - 1))
            return ins
        return self.add("pe", fn, reads, [out])

    def act(self, out, in_, func, bias=None, scale=None):
        reads = [in_]
        kw = {}
        if bias is not None:
            if isinstance(bias, R):
                reads.append(bias)
                kw["bias"] = bias.ap
            else:
                kw["bias"] = bias
        if scale is not None:
            if isinstance(scale, R):
                reads.append(scale)
                kw["scale"] = scale.ap
            else:
                kw["scale"] = scale

        def fn(e, out=out, in_=in_, func=func, kw=kw):
            return e.activation(out=out.ap, in_=in_.ap, func=func, **kw)
        return self.add("act", fn, reads, [out])

    def tt(self, out, in0, in1, op, eng="dve"):
        def fn(e, out=out, in0=in0, in1=in1, op=op):
            return e.tensor_tensor(out=out.ap, in0=in0.ap, in1=in1.ap, op=op)
        return self.add(eng, fn, [in0, in1], [out])

    def ts(self, out, in0, s1, s2, op0, op1=None, eng="dve"):
        reads = [in0]
        a1 = s1.ap if isinstance(s1, R) else s1
        a2 = s2.ap if isinstance(s2, R) else s2
        if isinstance(s1, R):
            reads.append(s1)
        if isinstance(s2, R):
            reads.append(s2)

        def fn(e, out=out, in0=in0, a1=a1, a2=a2, op0=op0, op1=op1):
            if op1 is None:
                return e.tensor_scalar(out=out.ap, in0=in0.ap, scalar1=a1, scalar2=None, op0=op0)
            return e.tensor_scalar(out=out.ap, in0=in0.ap, scalar1=a1, scalar2=a2, op0=op0, op1=op1)
        return self.add(eng, fn, reads, [out])

    def stt(self, out, in0, scalar, in1, op0, op1, eng="dve"):
        reads = [in0, in1]
        a = scalar.ap if isinstance(scalar, R) else scalar
        if isinstance(scalar, R):
            reads.append(scalar)

        def fn(e, out=out, in0=in0, a=a, in1=in1, op0=op0, op1=op1):
            return e.scalar_tensor_tensor(out=out.ap, in0=in0.ap, scalar=a, in1=in1.ap, op0=op0, op1=op1)
        return self.add(eng, fn, reads, [out])

    def recip(self, out, in_):
        def fn(e, out=out, in_=in_):
            return e.reciprocal(out=out.ap, in_=in_.ap)
        return self.add("dve", fn, [in_], [out])

    def copy(self, out, in_, eng="dve"):
        def fn(e, out=out, in_=in_):
            return e.tensor_copy(out=out.ap, in_=in_.ap)
        return self.add(eng, fn, [in_], [out])

    def rmax(self, out, in_):
        def fn(e, out=out, in_=in_):
            return e.reduce_max(out=out.ap, in_=in_.ap, axis=AX.X)
        return self.add("dve", fn, [in_], [out])

    def memset(self, out, val, eng="dve"):
        def fn(e, out=out, val=val):
            return e.memset(out.ap, val)
        return self.add(eng, fn, [], [out])

    def dma_in(self, eng, out, src_ap, key, dram=None, **kw):
        def fn(e, out=out, src_ap=src_ap, kw=kw):
            src = src_ap(self.env) if callable(src_ap) else src_ap
            return e.dma_start(out=out.ap, in_=src, **kw)
        return self.add(eng, fn, [dram] if dram is not None else [], [out], dma_key=key)

    def dma_out(self, eng, dst_ap, in_, key, final=True, dram=None, **kw):
        def fn(e, dst_ap=dst_ap, in_=in_, kw=kw):
            return e.dma_start(out=dst_ap, in_=in_.ap, **kw)
        op = self.add(eng, fn, [in_], [dram] if dram is not None else [], dma_key=key, force=True)
        if final:
            self.final_deps.append(op)
        return op

    def dreg(self, name, chunk, n=1, whole=False):
        if whole:
            return R(None, "dram_" + name, chunk * 100000, (chunk + n) * 100000)
        self.ndram += 1
        return R(None, "dram_" + name, chunk * 100000 + self.ndram, chunk * 100000 + self.ndram + 1)

    def collective(self, kind, groups, in_ap, out_ap, in_reg, out_reg, key):
        def fn(e, kind=kind, groups=groups, in_ap=in_ap, out_ap=out_ap):
            return e.collective_compute(kind, ALU.bypass, replica_groups=groups, ins=[in_ap], outs=[out_ap])
        return self.add("pool", fn, [in_reg], [out_reg], dma_key=key, inc=1)

    def emit(self):
        nc = self.nc
        st = self.stack
        esem = {e: st.enter_context(nc.semaphore("s_" + e)) for e in ENGS}
        dsem = {}
        cnt = {e: 0 for e in ENGS}
        dcnt = {}
        fin = Op("sp", None, None)
        fin.deps = list(self.final_deps)
        self.ops.append(fin)
        for op in self.ops:
            if op.dma_key is not None:
                if op.dma_key not in dsem:
                    dsem[op.dma_key] = st.enter_context(nc.semaphore("d_" + str(op.dma_key)))
                    dcnt[op.dma_key] = 0
                dcnt[op.dma_key] += op.inc
                op.signal = (dsem[op.dma_key], dcnt[op.dma_key])
            elif op.signaled:
                cnt[op.eng] += 1
                op.signal = (esem[op.eng], cnt[op.eng])
        per = {e: [o for o in self.ops if o.eng == e] for e in ENGS}
        self.stats = {e: len(per[e]) for e in ENGS}
        nwaits = {e: 0 for e in ENGS}

        def run(e, name):
            waited = {}
            for op in per[name]:
                need = {}
                for d in op.deps:
                    sem, val = d.signal
                    k = id(sem)
                    if waited.get(k, 0) >= val:
                        continue
                    if k not in need or need[k][1] < val:
                        need[k] = (sem, val)
                for k, (sem, val) in need.items():
                    e.wait_ge(sem, val)
                    waited[k] = val
                    nwaits[name] += 1
                if op.fn is None:
                    continue
                ins = op.fn(e)
                if op.signal is not None:
                    sem, val = op.signal
                    ins.then_inc(sem, op.inc if op.dma_key is not None else 1)
        with nc.Block() as block:
            @block.tensor
            def _(e):
                run(e, "pe")

            @block.scalar
            def _(e):
                run(e, "act")

            @block.vector
            def _(e):
                run(e, "dve")

            @block.gpsimd
            def _(e):
                run(e, "pool")

            @block.sync
            def _(e):
                if self.sp_setup is not None:
                    self.sp_setup(e, self.env)
                run(e, "sp")
        self.stats["waits"] = nwaits
        st.close()
        return nc


class Tile:
    def __init__(self, K, seg, tt, halo=False):
        self.K, self.seg, self.tt, self.halo = K, seg, tt, halo
        self.n = 2 if halo else 512

    def h(self, c):
        K = self.K
        if self.halo:
            return K.HH.s(c * 2, 2)
        return K.H.s((self.seg * 8 + c) * SEG + self.tt * 512, 512)

    def xn(self, c):
        K = self.K
        if self.halo:
            return K.XNH.s(c * 2, 2)
        return K.XN.s(c * SEG + self.tt * 512, 512)

    def y(self, m):
        K = self.K
        if self.halo:
            return K.YH.s(m * 2, 2)
        return K.XY.s(m * SEG + self.tt * 512, 512)

    def hid(self, j):
        K = self.K
        if self.halo:
            return K.HIDH.s(j * 2, 2)
        return K.HID.s(j * SEG + self.tt * 512, 512)

    def rs(self):
        K = self.K
        if self.halo:
            return K.RSH.s(0, 2)
        return K.RS.s((self.seg * 2 + self.tt) * 512, 512)


class Kern:
    def __init__(self, mode):
        self.mode = mode
        self.P = Prog()
        self.nc = self.P.nc
        self.dram = {}
        self.wcount = 0
        self.tailq = []
        self.sqi = 0
        self.sgi = 0
        self.bki = 0
        self.ssi = 0

    def din(self, name, shape, dtype=F32):
        t = self.nc.dram_tensor(name, list(shape), dtype, kind="ExternalInput")
        self.dram[name] = t
        return t.ap()

    def dout(self, name, shape, dtype=F32):
        t = self.nc.dram_tensor(name, list(shape), dtype, kind="ExternalOutput")
        self.dram[name] = t
        return t.ap()

    nring = 4

    def drain(self, n=None):
        q = self.tailq
        k = len(q) if n is None else min(n, len(q))
        for _ in range(k):
            q.pop(0)()

    def bank(self):
        b = self.PS[self.bki % self.nring]
        self.bki += 1
        return b

    def ssbank(self):
        b = self.PS[4 + self.ssi % 4]
        self.ssi += 1
        return b

    def sq(self, n=512):
        r = self.SQ.s((self.sqi % 4) * 512, n)
        self.sqi += 1
        return r

    def sg(self, n=512):
        r = self.SG.s((self.sgi % 4) * 512, n)
        self.sgi += 1
        return r

    def wload(self, src_ap, nelem):
        slot = self.wcount % NWSLOT
        self.wcount += 1
        dst = self.W.s(slot * WSLOT, nelem)
        self.P.dma_in("pool", dst, src_ap, "w%d" % slot, max_dma_last_dim=8192)
        base = slot * WSLOT
        W = self.W
        return lambda lo, n, p0=0, p1=128: W.s(base + lo, n, None, p0, p1)

    def alloc_common(self):
        P = self.P
        self.H = P.sbuf("H", NSEG * 8 * SEG, F32)
        self.XY = P.sbuf("XY", 8 * SEG, F32)
        self.HID = P.sbuf("HID", NJ * SEG, BF16)
        self.W = P.sbuf("W", NWSLOT * WSLOT, BF16)
        self.SQ = P.sbuf("SQ", 3 * 512, BF16)
        self.SG = P.sbuf("SG", 3 * 512, F32)
        self.RS = P.sbuf("RS", 2 * 512, F32)
        self.CB = P.sbuf("CB", 128 * 4, BF16)
        self.G = P.sbuf("G", 12 * 8 + 8 + 24 + 8, F32)
        self.G5 = P.sbuf("G5", 4 * 8, F32)
        self.PS = [P.psum("ps%d" % i) for i in range(8)]
        self.ONES_D = self.CB.s(0, 128)
        self.ONES_H = self.CB.s(128, 128)
        self.ONES1 = self.CB.s(256, 128)
        self.IDENT = self.CB.s(384, 128)

    def load_consts(self, gains_ap, cb_ap):
        P = self.P
        P.dma_in("sp", self.G.s(0, 12 * 8 + 8 + 24), gains_ap, "cg")
        P.dma_in("sp", self.CB.s(0, 512), cb_ap, "ccb")
        for l in range(2):
            for i, gi in enumerate((1, 5)):
                P.ts(self.G5.s((l * 2 + i) * 8, 8), self.G.s((l * 6 + gi) * 8, 8), 0.5, None, ALU.mult)

    def gcol(self, l, i, c):
        return self.G.s((l * 6 + i) * 8 + c, 1)

    def prenorm(self, tiles, gfn):
        P = self.P
        for t in tiles:
            n = t.n
            bank = self.ssbank()
            for c in range(8):
                sq = self.sq(n)
                P.act(sq, t.h(c), AF.Square)
                P.mm(bank.s(0, n), [(self.ONES_D, sq)], start=(c == 0), stop=(c == 7))
            P.act(t.rs(), bank.s(0, n), AF.Ln, bias=self.EPSC, scale=1.0)
            P.act(t.rs(), t.rs(), AF.Exp, scale=-0.5)
            for c in range(8):
                P.stt(t.xn(c), t.h(c), gfn(c), t.rs(), ALU.mult, ALU.mult)

    def proj_out(self, tiles, src, nk, wsrc, gfn):
        P = self.P
        ssb = [self.ssbank() for _ in tiles]
        pending = None
        for m in range(8):
            w = self.wload(wsrc(m), nk * 128)
            for ti, t in enumerate(tiles):
                n = t.n
                py = self.bank().s(0, n)
                P.mm(py, [(w(k * 128, 128), src(k, t)) for k in range(nk)])
                if pending is not None:
                    P.mm(*pending[0], **pending[1])
                P.copy(t.y(m), py)
                sq = self.sq(n)
                P.act(sq, t.y(m), AF.Square)
                pending = ((ssb[ti].s(0, n), [(self.ONES_D, sq)]), dict(start=(m == 0), stop=(m == 7)))
        P.mm(*pending[0], **pending[1])
        for ti, t in enumerate(tiles):
            n = t.n
            P.act(t.rs(), ssb[ti].s(0, n), AF.Ln, bias=self.EPSC, scale=1.0)
            P.act(t.rs(), t.rs(), AF.Exp, scale=-0.5)
        for m in range(8):
            for ti, t in enumerate(tiles):
                def work(m=m, t=t):
                    tmp = self.sg(t.n)
                    P.stt(tmp, t.y(m), gfn(m), t.rs(), ALU.mult, ALU.mult)
                    P.tt(t.h(m), t.h(m), tmp, ALU.add)
                self.tailq.append(work)

    def ffn(self, tiles, f, wgu_ap, wdn_ap):
        P = self.P
        l, i = f // 2, f % 2
        self.prenorm(tiles, lambda c: self.gcol(l, 0 if i == 0 else 4, c))
        yield
        for j in range(NJ):
            w = self.wload(wgu_ap[f, j], 2048)
            for t in tiles:
                n = t.n
                pg = self.bank().s(0, n)
                pu = self.bank().s(0, n)
                P.mm(pg, [(w(c * 256, 128), t.xn(c)) for c in range(8)])
                P.mm(pu, [(w(c * 256 + 128, 128), t.xn(c)) for c in range(8)])
                sg = self.sg(n)
                P.act(sg, pg, AF.Silu)
                P.tt(t.hid(j), sg, pu, ALU.mult)
            self.drain(1)
        yield
        self.proj_out(tiles, lambda k, t: t.hid(k), NJ, lambda m: wdn_ap[f, m],
                      lambda m: self.G5.s(f * 8 + m, 1))

    def conv_mixer(self, seg, tiles, halo, win_ap, wout_ap):
        P = self.P
        allt = ([halo] if halo is not None else []) + tiles
        self.prenorm(allt, lambda c: self.gcol(0, 2, c))
        yield
        kc = lambda w, m: self.G.s(12 * 8 + 8 + w * 8 + m, 1)
        self.nring = 8
        for m in range(8):
            w = self.wload(win_ap[m], 3072)
            U = lambda lo, n, m=m: self.U.s((m % 2) * 1026 + lo, n)
            if halo is not None:
                pc = self.bank().s(0, 2)
                pz = self.bank().s(0, 2)
                P.mm(pc, [(w(c * 384 + 128, 128), halo.xn(c)) for c in range(8)])
                P.mm(pz, [(w(c * 384 + 256, 128), halo.xn(c)) for c in range(8)])
                cs = self.sg(2)
                P.act(cs, pc, AF.Copy)
                P.tt(U(0, 2), cs, pz, ALU.mult)
            else:
                P.copy(U(0, 2), self.UH.s(m * 2, 2))
            for t in tiles:
                tt = t.tt
                pb = self.bank().s(0, 512)
                pc = self.bank().s(0, 512)
                pz = self.bank().s(0, 512)
                P.mm(pb, [(w(c * 384, 128), t.xn(c)) for c in range(8)])
                P.mm(pc, [(w(c * 384 + 128, 128), t.xn(c)) for c in range(8)])
                P.mm(pz, [(w(c * 384 + 256, 128), t.xn(c)) for c in range(8)])
                cs = self.sg()
                P.act(cs, pc, AF.Copy)
                P.tt(U(2 + tt * 512, 512), cs, pz, ALU.mult)
                t1 = self.sg()
                P.act(t1, U(tt * 512 + 2, 512), AF.Copy, scale=kc(2, m))
                P.stt(t1, U(tt * 512 + 1, 512), kc(1, m), t1, ALU.mult, ALU.add)
                P.stt(t1, U(tt * 512, 512), kc(0, m), t1, ALU.mult, ALU.add)
                P.tt(t.hid(m), t1, pb, ALU.mult)
            if seg == 0:
                P.copy(self.UH.s(m * 2, 2), U(1024, 2))
            self.drain(2)
        self.nring = 4
        yield
        self.proj_out(tiles, lambda k, t: t.hid(k), 8, lambda m: wout_ap[m],
                      lambda m: self.gcol(0, 3, m))

    def qk_proj(self, tiles, seg, w_ap, dst_ap, after_hl=None):
        P = self.P
        for hi in range(8):
            hl, r = hi // 2, hi % 2
            hd = HEAD_GROUPS[r][hl]
            w = self.wload(w_ap[hd], 1024)
            for t in tiles:
                pk = self.bank().s(0, 512)
                P.mm(pk, [(w(c * 128, 128), t.xn(c)) for c in range(8)])
                slot = self.ksi % 3
                self.ksi += 1
                ksb = self.KSB.s(slot * 512, 512)
                P.act(ksb, pk, AF.Copy)
                c0 = seg * SEG + t.tt * 512
                P.dma_out("sp", dst_ap(r, hl)[:, c0:c0 + 512], ksb, "ks%d" % slot, final=False,
                          dram=P.dreg("xs", hl * 2 + r))
            self.drain(2)
            if after_hl is not None and r == 1:
                after_hl(hl)


PAIRS = [[0, 1], [2, 3], [4, 5], [6, 7]]


def build_F():
    K = Kern("F")
    P = K.P
    nc = K.nc
    xT = K.din("xT", [128, NSEG * 8 * SEG])
    xh = K.din("xh", [128, 16])
    gains = K.din("gains", [128, 128])
    cb = K.din("cb", [128, 512], BF16)
    cb2 = K.din("cb2", [128, 384], BF16)
    oh_in = K.din("oh", [128, 2])
    wgu = K.din("wgu", [4, NJ, 128, 2048])
    wdn = K.din("wdn", [4, 8, 128, DFF])
    win = K.din("win", [8, 128, 3072])
    wout = K.din("wout", [8, 128, 1024])
    wk = K.din("wk", [8, 128, 1024])
    wv = K.din("wv", [4, 128, 2048])
    wq = K.din("wq", [8, 128, 1024])
    wo = K.din("wo", [8, 128, 1024])
    alibi = K.din("alibi", [128, 4 * NBIAS])
    tq = K.din("tq", [4, S], BF16)
    lamp = K.din("lamp", [128, 256])
    subg = K.din("subg", [128, 1])
    rank_in = K.din("rank", [1, 1], mybir.dt.int32)
    out = K.dout("out", [128, NSEG * 8 * SEG])
    xs = nc.dram_tensor("xs", [2 * 3 * 4 * 128, TOK], BF16)
    xg = nc.dram_tensor("xg", [2 * 2 * 3 * 4 * 128, TOK], BF16)
    os_ = nc.dram_tensor("os", [2 * 4 * 128, TOK], BF16)
    og = nc.dram_tensor("og", [2 * 2 * 4 * 128, TOK], BF16)
    xs5 = xs.ap().rearrange("(h r k p) t -> h r k p t", h=4, r=2, k=3)
    kt_o = lambda r, hl: xs5[hl, r, 0]
    qt_o = lambda r, hl: xs5[hl, r, 1]
    v_o = lambda r: xs5[:, r, 2].rearrange("h p (b e) -> h p b e", b=16)
    xg6 = xg.ap().rearrange("(h r s k p) t -> h r s k p t", h=4, r=2, s=2, k=3)
    os4 = os_.ap().rearrange("(h d p) t -> h d p t", h=4, d=2)
    og5 = og.ap().rearrange("(h s d p) t -> h s d p t", h=4, s=2, d=2)

    def sp_setup(e, env):
        reg = e.alloc_register("rank")
        e.reg_load(reg, rank_in[0:1, 0:1])
        env["rank"] = e.snap(reg, min_val=0, max_val=1)
    P.sp_setup = sp_setup

    XH = P.sbuf("XH", 19456, F32)
    K.H = P.sbuf("H", NSEG * 8 * SEG, F32)
    K.XY = SubBuf(XH, 0, F32)
    AUX = P.sbuf("AUX", 13832, BF16)
    K.XN = SubBuf(AUX, 0, BF16)
    K.HID = SubBuf(XH, 32768, BF16)
    K.W = P.sbuf("W", NWSLOT * WSLOT, BF16)
    K.SQ = P.sbuf("SQ", 4 * 512, BF16)
    K.SG = P.sbuf("SG", 4 * 512, F32)
    K.RS = P.sbuf("RS", 4 * 512, F32)
    K.CB = P.sbuf("CB", 512, BF16)
    K.G = P.sbuf("G", 136, F32)
    K.G5 = P.sbuf("G5", 32, F32)
    PP = [P.psum("pp%d" % i, 1024) for i in range(4)]
    K.PS = [SubBuf(PP[i // 2], (i % 2) * 2048, F32) for i in range(8)]
    K.ONES_D = K.CB.s(0, 128)
    K.ONES_H = K.CB.s(128, 128)
    K.ONES1 = K.CB.s(256, 128)
    K.IDENT = K.CB.s(384, 128)
    K.U = SubBuf(AUX, 16384, F32)
    K.UH = P.sbuf("UH", 16, F32)
    K.HH = P.sbuf("HH", 16, F32)
    K.XNH = P.sbuf("XNH", 16, BF16)
    K.YH = P.sbuf("YH", 16, F32)
    K.HIDH = P.sbuf("HIDH", NJ * 2, BF16)
    K.RSH = P.sbuf("RSH", 2, F32)
    K.KSB = SubBuf(AUX, 24592, BF16)
    K.EPSB = P.sbuf("EPSB", 1, F32)
    K.ksi = 0
    RAW = [SubBuf(XH, 0, BF16), SubBuf(AUX, 0, BF16)]
    KZ = [SubBuf(XH, 24576, BF16), SubBuf(XH, 32768, BF16)]
    QZ = [SubBuf(XH, 40960, BF16), SubBuf(XH, 49152, BF16)]
    PR = SubBuf(XH, 57344, BF16)
    AT = SubBuf(XH, 63488, F32)
    OB = SubBuf(XH, 71680, BF16)
    CB2 = P.sbuf("CB2", 384, BF16)
    OH = P.sbuf("OH", 8, F32)
    AL = SubBuf(K.RS, 0, F32)
    BI = SubBuf(K.RS, 576, F32)
    LM = SubBuf(K.RS, 1152, F32)
    SUBG = SubBuf(K.RS, 2752, F32)
    MX = SubBuf(K.RS, 2816, F32)
    MASKNEG = CB2.s(0, 128)
    E = [CB2.s(128, 128), CB2.s(256, 128)]
    PS = K.PS
    SQ = K.SQ
    ONES_H, ONES1, IDENT = K.ONES_H, K.ONES1, K.IDENT

    P.memset(K.EPSB.s(0, 1), EPS)
    K.EPSC = K.EPSB.s(0, 1)
    K.load_consts(gains[:, 0:128], cb)
    xops = []
    x4 = xT.rearrange("p (s c t) -> p s c t", s=NSEG, c=8)
    for seg in range(NSEG):
        for tt in range(2):
            if seg == 1 and tt == 0:
                P.dma_in("sp", K.HH.s(0, 16), xh, "xh")
            dstr = K.H.s(seg * 8 * SEG, 8 * SEG)
            dst3 = R(dstr.ap.rearrange("p (c t) -> p c t", c=8)[:, :, tt * 512:(tt + 1) * 512], dstr.buf,
                     dstr.lo, dstr.hi)
            srcx = x4[:, seg, :, tt * 512:(tt + 1) * 512]

            def fn(e, dst3=dst3, srcx=srcx):
                return e.dma_start(out=dst3.ap, in_=srcx)
            op = P.add("sp", fn, [], [K.H.s((seg * 8 + c) * SEG + tt * 512, 512) for c in range(8)],
                       dma_key="x%d" % (seg * 2 + tt))
            op.deps += xops
            xops = [op]
    P.dma_in("sp", CB2.s(0, 384), cb2, "c1")
    P.dma_in("sp", OH.s(0, 2), oh_in, "c6")

    TL = [[Tile(K, seg, 0), Tile(K, seg, 1)] for seg in range(NSEG)]
    HALO = [Tile(K, 0, 0, halo=True), None]

    def st_ffn0(seg):
        return K.ffn(([HALO[seg]] if HALO[seg] else []) + TL[seg], 0, wgu, wdn)

    def st_conv(seg):
        return K.conv_mixer(seg, TL[seg], HALO[seg], win, wout)

    def st_ffn1(seg):
        return K.ffn(TL[seg], 1, wgu, wdn)

    def st_kv(seg):
        tiles = TL[seg]
        K.prenorm(tiles, lambda c: K.G.s(12 * 8 + c, 1))
        yield
        K.qk_proj(tiles, seg, wk, kt_o)
        for n4 in range(4):
            w = K.wload(wv[n4], 2048)
            for tb in range(8):
                t = tiles[tb // 4]
                pv = K.bank().s(0, 256)
                P.mm(pv, [(K.XN.s(c * SEG + t.tt * 512 + (tb % 4) * 128, 128), w(c * 256, 256))
                          for c in range(8)])
                P.act(K.HID.s(tb * 1024 + n4 * 256, 256), pv, AF.Copy)
        for tb in range(8):
            for r in range(2):
                src = K.HID.s(tb * 1024 + r * 512, 512)
                dst = v_o(r)[:, :, seg * 8 + tb, :].rearrange("h p e -> p h e")
                src3 = R(src.ap.rearrange("p (h e) -> p h e", h=4), src.buf, src.lo, src.hi)
                op = P.dma_out("sp", dst, src3, "vs%d" % (tb * 2 + r), final=False, dram=P.dreg("xs", r))
                for hl in range(1, 4):
                    P.acc.setdefault("dram_xs", {"w": [], "r": []})["w"].append(
                        (P.dreg("xs", hl * 2 + r).lo, P.dreg("xs", hl * 2 + r).hi, op))
        yield

    def st_ffn2(seg):
        return K.ffn(TL[seg], 2, wgu, wdn)

    def st_q(seg):
        K.prenorm(TL[seg], lambda c: K.gcol(1, 2, c))
        yield
        def exch1(hl):
            for ci in (hl * 2, hl * 2 + 1):
                P.collective("AllGather", PAIRS, xs.ap()[ci * 384:(ci + 1) * 384, :].opt(),
                             xg.ap()[ci * 768:(ci + 1) * 768, :].opt(), P.dreg("xs", ci, whole=True),
                             P.dreg("xg", ci, whole=True), "ccx%d" % ci)

        def exch(hl):
            if hl >= 1:
                exch1(hl - 1)
            if hl == 3:
                exch1(3)
        K.qk_proj(TL[seg], seg, wq, qt_o, after_hl=(exch if seg == NSEG - 1 else None))
        yield

    def run_pipeline(stages):
        gens = [st(seg) for st in stages for seg in range(NSEG)]
        next(gens[0])
        for i, g in enumerate(gens):
            next(g)
            K.drain()
            if i + 1 < len(gens):
                next(gens[i + 1])
            for _ in g:
                pass
        K.drain()

    run_pipeline((st_ffn0, st_conv, st_ffn1, st_kv, st_ffn2, st_q))

    P.dma_in("sp", AL.s(0, 4 * NBIAS), alibi, "c3")
    P.dma_in("sp", LM.s(0, 256), lamp, "c4")
    P.dma_in("sp", SUBG.s(0, 1), subg, "c5")
    P.memset(KZ[0].s(0, S, None, 64, 128), 0.0)
    P.memset(KZ[0].s(0, S, None, 64, 65), 1.0)
    P.memset(KZ[1].s(0, S, None, 0, 64), 0.0)
    P.memset(KZ[1].s(0, S, None, 0, 1), 1.0)
    P.memset(QZ[0].s(0, S, None, 64, 128), 0.0)
    P.memset(QZ[1].s(0, S, None, 0, 64), 0.0)
    for i in range(2):
        P.tt(LM.s(256 + i * 64, 64), LM.s(i * 128, 64), LM.s(i * 128 + 64, 64), ALU.mult)

        def fn(e, i=i):
            return e.reduce_sum(out=LM.s(384 + i, 1).ap, in_=LM.s(256 + i * 64, 64).ap, axis=AX.X)
        P.add("dve", fn, [LM.s(256 + i * 64, 64)], [LM.s(384 + i, 1)])
        P.act(LM.s(386 + i, 1), LM.s(384 + i, 1), AF.Exp)
    P.tt(LM.s(388, 1), LM.s(387, 1), LM.s(386, 1), ALU.subtract)
    P.ts(LM.s(389, 1), LM.s(388, 1), -LAMBDA_INIT, None, ALU.add)
    NEGLAM = LM.s(389, 1)
    P.ts(SUBG.s(1, 1), SUBG.s(0, 1), 1.0 - LAMBDA_INIT, None, ALU.mult)
    SUBG2 = SUBG.s(1, 1)

    sqi = [0]
    sbi = [0]
    MSQ = SubBuf(XH, 73728, BF16)

    def head_load(hl):
        def srcfn(env, hl=hl):
            return xg6[hl, bass.ds(env["rank"], 1)].rearrange("o s k p t -> (o p) (s k) t")
        dstr = RAW[hl % 2].s(0, 3 * S)
        dst4 = R(dstr.ap.rearrange("p (sk t) -> p sk t", sk=6), dstr.buf, dstr.lo, dstr.hi)
        P.dma_in("sp", dst4, srcfn, "kv%d" % (hl % 2), dram=P.dreg("xg", hl * 2, 2, whole=True))

    EBD = E[0]
    head_load(0)
    for hl in range(4):
        slot = 0
        KV = RAW[hl % 2]
        if hl + 1 < 4:
            head_load(hl + 1)
        P.dma_in("sp", QZ[0].s(0, S, None, 64, 65), tq[hl:hl + 1, :], "tq0")
        P.dma_in("sp", QZ[1].s(0, S, None, 0, 1), tq[hl:hl + 1, :], "tq1")

        def kcol(kb):
            return (kb // 16) * 3 * TOK + (kb % 16) * 128

        def qcol(g):
            return (g // 4) * 3 * TOK + TOK + (g % 4) * 512

        def vcol(kb):
            return (kb // 16) * 3 * TOK + 2 * TOK + (kb % 16) * 128

        def bound_ops(h2):
            RW = RAW[h2 % 2]
            mo = (h2 % 2) * 24
            ops = []
            pend = []

            def op_all():
                tiles16 = [(xi, g) for xi in range(2) for g in range(8)]
                for t in range(18):
                    if t < 16:
                        xi, g = tiles16[t]
                        sq = MSQ.s((sqi[0] % 4) * 512, 512)
                        sqi[0] += 1
                        srcr = RW.s(qcol(g) if xi == 0 else kcol(4 * g), 512)
                        P.tt(sq, srcr, srcr, ALU.mult)
                        pend.append((xi, g, sq))
                    if t >= 2:
                        xi2, g2, sq2 = pend.pop(0)
                        bk = PS[sbi[0] % 4].s(0, 512)
                        sbi[0] += 1
                        P.mm(bk, [(EBD, sq2)])
                        P.rmax(MX.s(mo + xi2 * 8 + g2, 1), bk)
            ops.append(op_all)

            def fin():
                for xi in range(2):
                    P.rmax(MX.s(mo + 16 + xi, 1), MX.s(mo + xi * 8, 8))
                P.tt(MX.s(mo + 18, 1), MX.s(mo + 16, 1), MX.s(mo + 17, 1), ALU.mult)
                P.act(MX.s(mo + 19, 1), MX.s(mo + 18, 1), AF.Ln, scale=1.0 / 64.0)
                P.act(MX.s(mo + 19, 1), MX.s(mo + 19, 1), AF.Exp, scale=0.5)
                msel = MSQ.s((sqi[0] % 4) * 512, 2)
                sqi[0] += 1
                P.ts(msel, OH.s(0, 2), MX.s(mo + 19, 1), None, ALU.mult)
                bk = PS[sbi[0] % 4].s(0, 2)
                sbi[0] += 1
                P.mm(bk, [(ONES1, msel)])
                P.copy(MX.s(mo + 20, 2), bk)
                P.tt(MX.s(mo + 22, 1), MX.s(mo + 20, 1), MX.s(mo + 21, 1), ALU.max)
                P.ts(BI.s((h2 % 2) * NBIAS, NBIAS), AL.s(h2 * NBIAS, NBIAS), MX.s(mo + 22, 1), None, ALU.subtract)
            ops.append(fin)
            return ops

        for op in bound_ops(hl):
            op()
        slot = hl % 2
        mq = []

        for g in range(8):
            for c in range(2):
                p0, p1 = c * 64, c * 64 + 64
                P.copy(KZ[c].s(g * 512, 512, None, p0, p1), KV.s(kcol(4 * g), 512, None, p0, p1))
                P.act(QZ[c].s(g * 512, 512, None, p0, p1), KV.s(qcol(g), 512, None, p0, p1), AF.Copy)

        steps = [(g, kb) for g in range(8) for kb in range(4 * g + 4)]
        Ob = [PS[4], PS[5]]
        Lb = [PS[6], PS[7]]

        def emit_score(i):
            g, kb = steps[i]
            j = kb - 4 * g
            cs = max(j, 0) * 128
            ncol = 512 - cs
            sp = PP[i % 2]
            ops = []
            reads = []
            for c in range(2):
                kop = KZ[c].s(kb * 128, 128)
                qop = QZ[c].s(g * 512 + cs, ncol)
                reads += [kop, qop]
                ops.append((sp.s(c * 512 + cs, ncol), kop, qop, sp.s(c * 512 + cs, 128)))
            if j >= 0:
                reads += [IDENT, MASKNEG]

            def fn(e, ops=ops, j=j):
                ins = None
                for so, kop, qop, sm in ops:
                    ins = e.matmul(so.ap, lhsT=kop.ap, rhs=qop.ap, start=True, stop=(j < 0))
                    if j >= 0:
                        ins = e.matmul(sm.ap, lhsT=IDENT.ap, rhs=MASKNEG.ap, start=False, stop=True)
                return ins
            P.add("pe", fn, reads, [sp.s(0, 1024)])

        emit_score(0)
        deferred = []
        for i, (g, kb) in enumerate(steps):
            while deferred and deferred[0][0] <= i:
                deferred.pop(0)[1](i)
            nkb = 4 * g + 4
            j = kb - 4 * g
            cs = max(j, 0) * 128
            ncol = 512 - cs
            last = (kb == nkb - 1)
            spr = PP[i % 2].s(0, 1024)
            prr = PR.s((i % 3) * 1024, 1024)
            in3 = R(spr.ap.rearrange("p (c n) -> p c n", c=2)[:, :, cs:512], spr.buf, spr.lo, spr.hi)
            out3 = R(prr.ap.rearrange("p (c n) -> p c n", c=2)[:, :, cs:512], prr.buf, prr.lo, prr.hi)
            bias = BI.s(slot * NBIAS + (4 * g - kb + 3), 1)
            P.act(out3, in3, AF.Exp, bias=bias, scale=0.125)
            if i + 1 < len(steps):
                emit_score(i + 1)
            vop = KV.s(vcol(kb), 128)
            for c in range(2):
                pr = PR.s((i % 3) * 1024 + c * 512 + cs, ncol)
                P.mm(Ob[c].s(cs, ncol), [(vop, pr)], start=(kb == 0), stop=last)
                P.mm(Lb[c].s(cs, ncol), [(ONES1, pr)], start=(kb == 0), stop=last)
            if not last:
                continue
            AT4 = [(AT if g % 2 == 0 else K.SG).s(k * 512, 512) for k in range(4)]
            a0, a1, a2, a3 = AT4
            P.copy(a0, Lb[0].s(0, 512))
            P.act(a1, Ob[0].s(0, 512), AF.Copy)
            P.copy(a2, Lb[1].s(0, 512))
            P.act(a3, Ob[1].s(0, 512), AF.Copy)
            P.recip(a0, a0)
            P.tt(a1, a1, a0, ALU.mult)
            P.recip(a2, a2)
            P.tt(a3, a3, a2, ALU.mult)
            P.stt(a1, a3, NEGLAM, a1, ALU.mult, ALU.add)
            sq = MSQ.s((sqi[0] % 4) * 512, 512)
            sqi[0] += 1
            P.tt(sq, a1, a1, ALU.mult)

            def tail(icur, g=g, a0=a0, a1=a1, sq=sq):
                bk = PP[(icur + 1) % 2].s(0, 512)
                P.mm(bk, [(ONES_H, sq)])
                P.act(a0, bk, AF.Ln, bias=K.EPSC, scale=1.0)
                P.act(a0, a0, AF.Exp, scale=-0.5)
                ob = OB.s((g % 2) * 512, 512)
                P.stt(ob, a1, SUBG2, a0, ALU.mult, ALU.mult)
                P.dma_out("sp", os4[hl, g // 4, :, (g % 4) * 512:(g % 4) * 512 + 512], ob, "ob%d" % (g % 2),
                          final=False, dram=P.dreg("os", hl))
            deferred.append((i + 12, tail))
        while deferred:
            deferred.pop(0)[1](len(steps))
        while mq:
            mq.pop(0)()
        P.collective("AllGather", PAIRS, os_.ap()[hl * 256:(hl + 1) * 256, :].opt(),
                     og.ap()[hl * 512:(hl + 1) * 512, :].opt(), P.dreg("os", hl, whole=True),
                     P.dreg("og", hl, whole=True), "cco%d" % hl)

    ogreg = P.dreg("og", 0, 4, whole=True)

    def st_wo(seg):
        def srcfn(env, seg=seg):
            return og5[:, :, bass.ds(env["rank"], 1), :, seg * SEG:(seg + 1) * SEG].rearrange(
                "h s o p t -> (o p) (h s) t")
        dstr = K.HID.s(0, 8 * SEG)
        dst3 = R(dstr.ap.rearrange("p (c t) -> p c t", c=8), dstr.buf, dstr.lo, dstr.hi)
        yield
        P.dma_in("sp", dst3, srcfn, "at", dram=ogreg)
        yield
        K.proj_out(TL[seg], lambda k, t: t.hid(k), 8, lambda m: wo[m], lambda m: K.gcol(1, 3, m))

    def st_ffn3(seg):
        return K.ffn(TL[seg], 3, wgu, wdn)

    run_pipeline((st_wo, st_ffn3))
    for q in range(4):
        n = NSEG * 8 * SEG // 4
        P.dma_out("sp", out[:, q * n:(q + 1) * n], K.H.s(q * n, n), "ho%d" % q)
    nc = P.emit()
    return nc, P


def _tile_cols(w, cols):
    kk = w.shape[0] // 128
    t = w[:, cols].reshape(kk, 128, len(cols)).transpose(1, 0, 2)
    return np.ascontiguousarray(t).reshape(128, -1)


def _prep_weights(ffn_w_gu, ffn_w_down, conv_w_in, conv_w_out, w_kv, attn_w_q, attn_w_o):
    ar = np.arange
    wgu = np.empty((4, NJ, 128, 2048), np.float32)
    wdn = np.empty((4, 8, 128, DFF), np.float32)
    for f in range(4):
        l, i = f // 2, f % 2
        g = ffn_w_gu[l, i]
        for j in range(NJ):
            cols = np.concatenate([ar(j * 128, j * 128 + 128), DFF + ar(j * 128, j * 128 + 128)])
            wgu[f, j] = _tile_cols(g, cols)
        dn = ffn_w_down[l, i]
        for m in range(8):
            wdn[f, m] = _tile_cols(dn, ar(m * 128, m * 128 + 128))
    win = np.empty((8, 128, 3072), np.float32)
    for m in range(8):
        cols = np.concatenate([ar(m * 128, m * 128 + 128), D + ar(m * 128, m * 128 + 128),
                               2 * D + ar(m * 128, m * 128 + 128)])
        win[m] = _tile_cols(conv_w_in[0], cols)
    sq = lambda w: np.stack([_tile_cols(w, ar(m * 128, m * 128 + 128)) for m in range(8)])
    wout = sq(conv_w_out[0])
    wk = sq(w_kv[:, :D])
    wq = sq(attn_w_q[0])
    perm = np.concatenate([D + ar(h * 128, h * 128 + 128) for g in HEAD_GROUPS for h in g])
    perm_o = np.concatenate([ar(HEAD_GROUPS[r][hl] * 128, HEAD_GROUPS[r][hl] * 128 + 128)
                             for hl in range(4) for r in range(2)])
    wo = sq(attn_w_o[0][perm_o])
    wv = np.stack([_tile_cols(w_kv, perm[n * 256:(n + 1) * 256]) for n in range(4)])
    return dict(wgu=wgu, wdn=wdn, win=win, wout=wout, wk=wk, wq=wq, wo=wo, wv=wv)


def _const_cb():
    cb = np.zeros((128, 512), np.float32)
    cb[:, 0:128] = 1.0 / D
    cb[:, 128:256] = 1.0 / 128.0
    cb[:, 256:384] = 1.0
    cb[:, 384:512] = np.eye(128, dtype=np.float32)
    return cb.astype(ml_dtypes.bfloat16)


def _const_cb2():
    cb2 = np.zeros((128, 384), np.float32)
    k = np.arange(128)[:, None]
    q = np.arange(128)[None, :]
    cb2[:, 0:128] = np.where(k > q, -30000.0, 0.0)
    cb2[0:64, 128:192] = 1.0
    cb2[64:128, 192:256] = 1.0
    cb2[64:128, 256:384] = 1.0
    return cb2.astype(ml_dtypes.bfloat16)


_CACHE = {}


def _get(name, fn):
    if name not in _CACHE:
        _CACHE[name] = fn()
    return _CACHE[name]


def make_maps(x, ffn_w_gu, ffn_w_down, norm_g, conv_w_in, conv_k, conv_w_out, kv_norm_g, w_kv,
              attn_w_q, attn_lambda, attn_subln_g, attn_w_o, cores=range(8)):
    f32 = np.float32
    x = np.asarray(x, f32)
    Wt = _prep_weights(np.asarray(ffn_w_gu, f32), np.asarray(ffn_w_down, f32), np.asarray(conv_w_in, f32),
                       np.asarray(conv_w_out, f32), np.asarray(w_kv, f32), np.asarray(attn_w_q, f32),
                       np.asarray(attn_w_o, f32))
    gains = np.zeros((128, 128), f32)
    gains[:, 0:96] = np.asarray(norm_g, f32).reshape(2, 6, 8, 128).transpose(3, 0, 1, 2).reshape(128, 96)
    gains[:, 96:104] = np.asarray(kv_norm_g, f32).reshape(8, 128).T
    gains[:, 104:128] = np.asarray(conv_k, f32)[0].reshape(3, 8, 128).transpose(2, 0, 1).reshape(128, 24)
    cb = _const_cb()
    cb2 = _const_cb2()
    slopes = 2.0 ** (-8.0 * np.arange(1, 9) / 8.0)
    lam_in = np.tile(np.asarray(attn_lambda, f32)[0].reshape(1, 256), (128, 1))
    subg = np.asarray(attn_subln_g, f32)[0].reshape(128, 1)
    ki = np.arange(128, dtype=np.float64)[:, None]
    npr = np.arange(NBIAS, dtype=np.float64)[None, :] - 3.0
    oh = np.zeros((128, 2), f32)
    oh[0, 0] = 1.0
    oh[64, 1] = 1.0
    maps = []
    for c in cores:
        b, half = c // 2, c % 2
        xs = x[b, half * TOK:(half + 1) * TOK]
        xT = xs.reshape(NSEG, SEG, 8, 128).transpose(3, 0, 2, 1).reshape(128, NSEG * 8 * SEG)
        if half == 0:
            xh = np.zeros((128, 16), f32)
        else:
            xh = x[b, TOK - 2:TOK].reshape(2, 8, 128).transpose(2, 1, 0).reshape(128, 16)
        al = np.zeros((128, 4, NBIAS), f32)
        tq = np.zeros((4, S), f32)
        for hl, hd in enumerate(HEAD_GROUPS[half]):
            al[:, hl, :] = slopes[hd] * (ki - 128.0 * npr)
            tq[hl, :] = -8.0 * slopes[hd] * (np.arange(S) % 512)
        maps.append(dict(xT=np.ascontiguousarray(xT), xh=np.ascontiguousarray(xh), gains=gains, cb=cb, cb2=cb2,
                         wgu=Wt["wgu"], wdn=Wt["wdn"], win=Wt["win"], wout=Wt["wout"], wk=Wt["wk"],
                         wv=Wt["wv"], wq=Wt["wq"], wo=Wt["wo"], alibi=al.reshape(128, 4 * NBIAS),
                         tq=tq.astype(ml_dtypes.bfloat16), lamp=lam_in, subg=subg,
                         rank=np.array([[half]], np.int32), oh=oh))
    return maps


def kernel(x, ffn_w_gu, ffn_w_down, norm_g, conv_w_in, conv_k, conv_w_out, kv_norm_g, w_kv,
           attn_w_q, attn_lambda, attn_subln_g, attn_w_o):
    maps = make_maps(x, ffn_w_gu, ffn_w_down, norm_g, conv_w_in, conv_k, conv_w_out, kv_norm_g, w_kv,
                     attn_w_q, attn_lambda, attn_subln_g, attn_w_o)
    nc, _ = _get("F", build_F)
    res = run_bass_kernel_spmd(nc, maps, core_ids=list(range(8))).results
    out = np.empty((NB, S, D), np.float32)
    for c in range(8):
        b, half = c // 2, c % 2
        o = res[c]["out"].reshape(128, NSEG, 8, SEG).transpose(1, 3, 2, 0).reshape(TOK, D)
        out[b, half * TOK:(half + 1) * TOK] = o
    return out
```

```python
import math
import contextlib
import numpy as np
import ml_dtypes
import concourse.bass as bass
import concourse.mybir as mybir
from concourse.bass_utils import run_bass_kernel_spmd

F32 = mybir.dt.float32
BF16 = mybir.dt.bfloat16
AF = mybir.ActivationFunctionType
ALU = mybir.AluOpType
AX = mybir.AxisListType

D = 1024
DFF = 2816
NJ = 22
SEG = 1024
NSEG = 2
TOK = 2048
S = 4096
NB = 4
EPS = 1e-6
HEAD_GROUPS = [[0, 2, 4, 5], [1, 3, 6, 7]]
LAMBDA_INIT = 0.8 - 0.6 * math.exp(-0.3 * 1)
WSLOT = 3072
NWSLOT = 3
NBIAS = 35
SZ = {F32: 4, BF16: 2}


class R:
    __slots__ = ("ap", "buf", "lo", "hi")

    def __init__(self, ap, buf, lo, hi):
        self.ap, self.buf, self.lo, self.hi = ap, buf, lo, hi


class Buf:
    def __init__(self, name, handle, dtype):
        self.name, self.h, self.dtype = name, handle, dtype
        self.views = {dtype: handle}

    def s(self, lo, n, dtype=None, p0=0, p1=128):
        dt = dtype or self.dtype
        if dt not in self.views:
            self.views[dt] = self.h.bitcast(dt)
        v = self.views[dt]
        sz = SZ[dt]
        return R(v[p0:p1, lo:lo + n], self.name, lo * sz, (lo + n) * sz)


class SubBuf:
    def __init__(self, parent, byte_off, dtype):
        self.parent, self.off, self.dtype = parent, byte_off, dtype
        self.name = parent.name

    def s(self, lo, n, dtype=None, p0=0, p1=128):
        dt = dtype or self.dtype
        return self.parent.s(self.off // SZ[dt] + lo, n, dt, p0, p1)


class Op:
    __slots__ = ("eng", "fn", "deps", "dma_key", "signal", "signaled", "idx", "inc")

    def __init__(self, eng, fn, dma_key, inc=16):
        self.eng, self.fn, self.dma_key, self.inc = eng, fn, dma_key, inc
        self.deps = []
        self.signal = None
        self.signaled = False


ENGS = ["pe", "act", "dve", "pool", "sp"]


class Prog:
    def __init__(self):
        self.nc = bass.Bass("TRN2", target_bir_lowering=False)
        self.ops = []
        self.acc = {}
        self.stack = contextlib.ExitStack()
        self.bufs = {}
        self.dma_count = {}
        self.final_deps = []
        self.last_dma = {}
        self.env = {}
        self.sp_setup = None
        self.ndram = 0
        self.nbank = 0
        self.nss = 0

    def sbuf(self, name, n, dtype):
        h = self.stack.enter_context(self.nc.sbuf_tensor(name, [128, n], dtype))
        b = Buf(name, h, dtype)
        self.bufs[name] = b
        return b

    def psum(self, name, n=512):
        h = self.stack.enter_context(self.nc.psum_tensor(name, [128, n], F32))
        b = Buf(name, h, F32)
        self.bufs[name] = b
        return b

    maxops = None

    def add(self, eng, fn, reads=(), writes=(), dma_key=None, force=False, inc=16):
        if Prog.maxops is not None and len(self.ops) >= Prog.maxops and not force:
            return None
        op = Op(eng, fn, dma_key, inc)
        op.idx = len(self.ops)
        deps = set()
        for r in reads:
            if r is None:
                continue
            a = self.acc.setdefault(r.buf, {"w": [], "r": []})
            for (lo, hi, o) in a["w"]:
                if lo < r.hi and r.lo < hi:
                    deps.add(o)
        for r in writes:
            a = self.acc.setdefault(r.buf, {"w": [], "r": []})
            for (lo, hi, o) in a["w"]:
                if lo < r.hi and r.lo < hi:
                    deps.add(o)
            for (lo, hi, o) in a["r"]:
                if lo < r.hi and r.lo < hi:
                    deps.add(o)
        for r in reads:
            if r is None:
                continue
            a = self.acc[r.buf]
            if eng in ("pe", "act", "dve"):
                a["r"] = [x for x in a["r"] if not (x[0] == r.lo and x[1] == r.hi and x[2].eng == eng
                                                      and x[2].dma_key is None)]
            a["r"].append((r.lo, r.hi, op))
        for r in writes:
            a = self.acc[r.buf]
            a["w"] = [x for x in a["w"] if not (r.lo <= x[0] and x[1] <= r.hi)]
            a["r"] = [x for x in a["r"] if not (r.lo <= x[0] and x[1] <= r.hi)]
            a["w"].append((r.lo, r.hi, op))
        deps.discard(op)
        for d in deps:
            if d.eng == "pe" and eng == "pe" and d.dma_key is None:
                continue
            op.deps.append(d)
            d.signaled = True
        if dma_key is not None:
            op.signaled = True
            prev = self.last_dma.get(dma_key)
            if prev is not None and prev not in op.deps:
                op.deps.append(prev)
            self.last_dma[dma_key] = op
        self.ops.append(op)
        return op

    def mm(self, out, pairs, start=True, stop=True):
        reads = []
        for l, r in pairs:
            reads += [l, r]
        n = len(pairs)

        def fn(e, out=out, pairs=pairs, start=start, stop=stop, n=n):
            ins = None
            for i, (l, r) in enumerate(pairs):
                ins = e.matmul(out.ap, lhsT=l.ap, rhs=r.ap, start=(start and i == 0),
                               stop=(stop and i == n - 1))
            return ins
        return self.add("pe", fn, reads, [out])

    def act(self, out, in_, func, bias=None, scale=None):
        reads = [in_]
        kw = {}
        if bias is not None:
            if isinstance(bias, R):
                reads.append(bias)
                kw["bias"] = bias.ap
            else:
                kw["bias"] = bias
        if scale is not None:
            if isinstance(scale, R):
                reads.append(scale)
                kw["scale"] = scale.ap
            else:
                kw["scale"] = scale

        def fn(e, out=out, in_=in_, func=func, kw=kw):
            return e.activation(out=out.ap, in_=in_.ap, func=func, **kw)
        return self.add("act", fn, reads, [out])

    def tt(self, out, in0, in1, op, eng="dve"):
        def fn(e, out=out, in0=in0, in1=in1, op=op):
            return e.tensor_tensor(out=out.ap, in0=in0.ap, in1=in1.ap, op=op)
        return self.add(eng, fn, [in0, in1], [out])

    def ts(self, out, in0, s1, s2, op0, op1=None, eng="dve"):
        reads = [in0]
        a1 = s1.ap if isinstance(s1, R) else s1
        a2 = s2.ap if isinstance(s2, R) else s2
        if isinstance(s1, R):
            reads.append(s1)
        if isinstance(s2, R):
            reads.append(s2)

        def fn(e, out=out, in0=in0, a1=a1, a2=a2, op0=op0, op1=op1):
            if op1 is None:
                return e.tensor_scalar(out=out.ap, in0=in0.ap, scalar1=a1, scalar2=None, op0=op0)
            return e.tensor_scalar(out=out.ap, in0=in0.ap, scalar1=a1, scalar2=a2, op0=op0, op1=op1)
        return self.add(eng, fn, reads, [out])

    def stt(self, out, in0, scalar, in1, op0, op1, eng="dve"):
        reads = [in0, in1]
        a = scalar.ap if isinstance(scalar, R) else scalar
        if isinstance(scalar, R):
            reads.append(scalar)

        def fn(e, out=out, in0=in0, a=a, in1=in1, op0=op0, op1=op1):
            return e.scalar_tensor_tensor(out=out.ap, in0=in0.ap, scalar=a, in1=in1.ap, op0=op0, op1=op1)
        return self.add(eng, fn, reads, [out])

    def recip(self, out, in_):
        def fn(e, out=out, in_=in_):
            return e.reciprocal(out=out.ap, in_=in_.ap)
        return self.add("dve", fn, [in_], [out])

    def copy(self, out, in_, eng="dve"):
        def fn(e, out=out, in_=in_):
            return e.tensor_copy(out=out.ap, in_=in_.ap)
        return self.add(eng, fn, [in_], [out])

    def rmax(self, out, in_):
        def fn(e, out=out, in_=in_):
            return e.reduce_max(out=out.ap, in_=in_.ap, axis=AX.X)
        return self.add("dve", fn, [in_], [out])

    def memset(self, out, val, eng="dve"):
        def fn(e, out=out, val=val):
            return e.memset(out.ap, val)
        return self.add(eng, fn, [], [out])

    def dma_in(self, eng, out, src_ap, key, dram=None, **kw):
        def fn(e, out=out, src_ap=src_ap, kw=kw):
            src = src_ap(self.env) if callable(src_ap) else src_ap
            return e.dma_start(out=out.ap, in_=src, **kw)
        return self.add(eng, fn, [dram] if dram is not None else [], [out], dma_key=key)

    def dma_out(self, eng, dst_ap, in_, key, final=True, dram=None, **kw):
        def fn(e, dst_ap=dst_ap, in_=in_, kw=kw):
            return e.dma_start(out=dst_ap, in_=in_.ap, **kw)
        op = self.add(eng, fn, [in_], [dram] if dram is not None else [], dma_key=key, force=True)
        if final:
            self.final_deps.append(op)
        return op

    def dreg(self, name, chunk, n=1, whole=False):
        if whole:
            return R(None, "dram_" + name, chunk * 100000, (chunk + n) * 100000)
        self.ndram += 1
        return R(None, "dram_" + name, chunk * 100000 + self.ndram, chunk * 100000 + self.ndram + 1)

    def collective(self, kind, groups, in_ap, out_ap, in_reg, out_reg, key):
        def fn(e, kind=kind, groups=groups, in_ap=in_ap, out_ap=out_ap):
            return e.collective_compute(kind, ALU.bypass, replica_groups=groups, ins=[in_ap], outs=[out_ap])
        return self.add("pool", fn, [in_reg], [out_reg], dma_key=key, inc=1)

    def emit(self):
        nc = self.nc
        st = self.stack
        esem = {e: st.enter_context(nc.semaphore("s_" + e)) for e in ENGS}
        dsem = {}
        cnt = {e: 0 for e in ENGS}
        dcnt = {}
        fin = Op("sp", None, None)
        fin.deps = list(self.final_deps)
        self.ops.append(fin)
        for op in self.ops:
            if op.dma_key is not None:
                if op.dma_key not in dsem:
                    dsem[op.dma_key] = st.enter_context(nc.semaphore("d_" + str(op.dma_key)))
                    dcnt[op.dma_key] = 0
                dcnt[op.dma_key] += op.inc
                op.signal = (dsem[op.dma_key], dcnt[op.dma_key])
            elif op.signaled:
                cnt[op.eng] += 1
                op.signal = (esem[op.eng], cnt[op.eng])
        per = {e: [o for o in self.ops if o.eng == e] for e in ENGS}
        self.stats = {e: len(per[e]) for e in ENGS}
        nwaits = {e: 0 for e in ENGS}

        def run(e, name):
            waited = {}
            for op in per[name]:
                need = {}
                for d in op.deps:
                    sem, val = d.signal
                    k = id(sem)
                    if waited.get(k, 0) >= val:
                        continue
                    if k not in need or need[k][1] < val:
                        need[k] = (sem, val)
                for k, (sem, val) in need.items():
                    e.wait_ge(sem, val)
                    waited[k] = val
                    nwaits[name] += 1
                if op.fn is None:
                    continue
                ins = op.fn(e)
                if op.signal is not None:
                    sem, val = op.signal
                    ins.then_inc(sem, op.inc if op.dma_key is not None else 1)
        with nc.Block() as block:
            @block.tensor
            def _(e):
                run(e, "pe")

            @block.scalar
            def _(e):
                run(e, "act")

            @block.vector
            def _(e):
                run(e, "dve")

            @block.gpsimd
            def _(e):
                run(e, "pool")

            @block.sync
            def _(e):
                if self.sp_setup is not None:
                    self.sp_setup(e, self.env)
                run(e, "sp")
        self.stats["waits"] = nwaits
        st.close()
        return nc


class Tile:
    def __init__(self, K, seg, tt, halo=False):
        self.K, self.seg, self.tt, self.halo = K, seg, tt, halo
        self.n = 2 if halo else 512

    def h(self, c):
        K = self.K
        if self.halo:
            return K.HH.s(c * 2, 2)
        return K.H.s((self.seg * 8 + c) * SEG + self.tt * 512, 512)

    def xn(self, c):
        K = self.K
        if self.halo:
            return K.XNH.s(c * 2, 2)
        return K.XN.s(c * SEG + self.tt * 512, 512)

    def y(self, m):
        K = self.K
        if self.halo:
            return K.YH.s(m * 2, 2)
        return K.XY.s(m * SEG + self.tt * 512, 512)

    def hid(self, j):
        K = self.K
        if self.halo:
            return K.HIDH.s(j * 2, 2)
        return K.HID.s(j * SEG + self.tt * 512, 512)

    def rs(self):
        K = self.K
        if self.halo:
            return K.RSH.s(0, 2)
        return K.RS.s((self.seg * 2 + self.tt) * 512, 512)


class Kern:
    def __init__(self, mode):
        self.mode = mode
        self.P = Prog()
        self.nc = self.P.nc
        self.dram = {}
        self.wcount = 0
        self.tailq = []
        self.sqi = 0
        self.sgi = 0
        self.bki = 0
        self.ssi = 0

    def din(self, name, shape, dtype=F32):
        t = self.nc.dram_tensor(name, list(shape), dtype, kind="ExternalInput")
        self.dram[name] = t
        return t.ap()

    def dout(self, name, shape, dtype=F32):
        t = self.nc.dram_tensor(name, list(shape), dtype, kind="ExternalOutput")
        self.dram[name] = t
        return t.ap()

    nring = 4

    def drain(self, n=None):
        q = self.tailq
        k = len(q) if n is None else min(n, len(q))
        for _ in range(k):
            q.pop(0)()

    def bank(self):
        b = self.PS[self.bki % self.nring]
        self.bki += 1
        return b

    def ssbank(self):
        b = self.PS[4 + self.ssi % 4]
        self.ssi += 1
        return b

    def sq(self, n=512):
        r = self.SQ.s((self.sqi % 4) * 512, n)
        self.sqi += 1
        return r

    def sg(self, n=512):
        r = self.SG.s((self.sgi % 4) * 512, n)
        self.sgi += 1
        return r

    def wload(self, src_ap, nelem):
        slot = self.wcount % NWSLOT
        self.wcount += 1
        dst = self.W.s(slot * WSLOT, nelem)
        self.P.dma_in("pool", dst, src_ap, "w%d" % slot, max_dma_last_dim=8192)
        base = slot * WSLOT
        W = self.W
        return lambda lo, n, p0=0, p1=128: W.s(base + lo, n, None, p0, p1)

    def alloc_common(self):
        P = self.P
        self.H = P.sbuf("H", NSEG * 8 * SEG, F32)
        self.XY = P.sbuf("XY", 8 * SEG, F32)
        self.HID = P.sbuf("HID", NJ * SEG, BF16)
        self.W = P.sbuf("W", NWSLOT * WSLOT, BF16)
        self.SQ = P.sbuf("SQ", 3 * 512, BF16)
        self.SG = P.sbuf("SG", 3 * 512, F32)
        self.RS = P.sbuf("RS", 2 * 512, F32)
        self.CB = P.sbuf("CB", 128 * 4, BF16)
        self.G = P.sbuf("G", 12 * 8 + 8 + 24 + 8, F32)
        self.G5 = P.sbuf("G5", 4 * 8, F32)
        self.PS = [P.psum("ps%d" % i) for i in range(8)]
        self.ONES_D = self.CB.s(0, 128)
        self.ONES_H = self.CB.s(128, 128)
        self.ONES1 = self.CB.s(256, 128)
        self.IDENT = self.CB.s(384, 128)

    def load_consts(self, gains_ap, cb_ap):
        P = self.P
        P.dma_in("sp", self.G.s(0, 12 * 8 + 8 + 24), gains_ap, "cg")
        P.dma_in("sp", self.CB.s(0, 512), cb_ap, "ccb")
        for l in range(2):
            for i, gi in enumerate((1, 5)):
                P.ts(self.G5.s((l * 2 + i) * 8, 8), self.G.s((l * 6 + gi) * 8, 8), 0.5, None, ALU.mult)

    def gcol(self, l, i, c):
        return self.G.s((l * 6 + i) * 8 + c, 1)

    def prenorm(self, tiles, gfn):
        P = self.P
        for t in tiles:
            n = t.n
            bank = self.ssbank()
            for c in range(8):
                sq = self.sq(n)
                P.act(sq, t.h(c), AF.Square)
                P.mm(bank.s(0, n), [(self.ONES_D, sq)], start=(c == 0), stop=(c == 7))
            P.act(t.rs(), bank.s(0, n), AF.Ln, bias=self.EPSC, scale=1.0)
            P.act(t.rs(), t.rs(), AF.Exp, scale=-0.5)
            for c in range(8):
                P.stt(t.xn(c), t.h(c), gfn(c), t.rs(), ALU.mult, ALU.mult)

    def proj_out(self, tiles, src, nk, wsrc, gfn):
        P = self.P
        ssb = [self.ssbank() for _ in tiles]
        pending = None
        for m in range(8):
            w = self.wload(wsrc(m), nk * 128)
            for ti, t in enumerate(tiles):
                n = t.n
                py = self.bank().s(0, n)
                P.mm(py, [(w(k * 128, 128), src(k, t)) for k in range(nk)])
                if pending is not None:
                    P.mm(*pending[0], **pending[1])
                P.copy(t.y(m), py)
                sq = self.sq(n)
                P.act(sq, t.y(m), AF.Square)
                pending = ((ssb[ti].s(0, n), [(self.ONES_D, sq)]), dict(start=(m == 0), stop=(m == 7)))
        P.mm(*pending[0], **pending[1])
        for ti, t in enumerate(tiles):
            n = t.n
            P.act(t.rs(), ssb[ti].s(0, n), AF.Ln, bias=self.EPSC, scale=1.0)
            P.act(t.rs(), t.rs(), AF.Exp, scale=-0.5)
        for m in range(8):
            for ti, t in enumerate(tiles):
                def work(m=m, t=t):
                    tmp = self.sg(t.n)
                    P.stt(tmp, t.y(m), gfn(m), t.rs(), ALU.mult, ALU.mult)
                    P.tt(t.h(m), t.h(m), tmp, ALU.add)
                self.tailq.append(work)

    def ffn(self, tiles, f, wgu_ap, wdn_ap):
        P = self.P
        l, i = f // 2, f % 2
        self.prenorm(tiles, lambda c: self.gcol(l, 0 if i == 0 else 4, c))
        yield
        for j in range(NJ):
            w = self.wload(wgu_ap[f, j], 2048)
            for t in tiles:
                n = t.n
                pg = self.bank().s(0, n)
                pu = self.bank().s(0, n)
                P.mm(pg, [(w(c * 256, 128), t.xn(c)) for c in range(8)])
                P.mm(pu, [(w(c * 256 + 128, 128), t.xn(c)) for c in range(8)])
                sg = self.sg(n)
                P.act(sg, pg, AF.Silu)
                P.tt(t.hid(j), sg, pu, ALU.mult)
            self.drain(1)
        yield
        self.proj_out(tiles, lambda k, t: t.hid(k), NJ, lambda m: wdn_ap[f, m],
                      lambda m: self.G5.s(f * 8 + m, 1))

    def conv_mixer(self, seg, tiles, halo, win_ap, wout_ap):
        P = self.P
        allt = ([halo] if halo is not None else []) + tiles
        self.prenorm(allt, lambda c: self.gcol(0, 2, c))
        yield
        kc = lambda w, m: self.G.s(12 * 8 + 8 + w * 8 + m, 1)
        self.nring = 8
        for m in range(8):
            w = self.wload(win_ap[m], 3072)
            U = lambda lo, n, m=m: self.U.s((m % 2) * 1026 + lo, n)
            if halo is not None:
                pc = self.bank().s(0, 2)
                pz = self.bank().s(0, 2)
                P.mm(pc, [(w(c * 384 + 128, 128), halo.xn(c)) for c in range(8)])
                P.mm(pz, [(w(c * 384 + 256, 128), halo.xn(c)) for c in range(8)])
                cs = self.sg(2)
                P.act(cs, pc, AF.Copy)
                P.tt(U(0, 2), cs, pz, ALU.mult)
            else:
                P.copy(U(0, 2), self.UH.s(m * 2, 2))
            for t in tiles:
                tt = t.tt
                pb = self.bank().s(0, 512)
                pc = self.bank().s(0, 512)
                pz = self.bank().s(0, 512)
                P.mm(pb, [(w(c * 384, 128), t.xn(c)) for c in range(8)])
                P.mm(pc, [(w(c * 384 + 128, 128), t.xn(c)) for c in range(8)])
                P.mm(pz, [(w(c * 384 + 256, 128), t.xn(c)) for c in range(8)])
                cs = self.sg()
                P.act(cs, pc, AF.Copy)
                P.tt(U(2 + tt * 512, 512), cs, pz, ALU.mult)
                t1 = self.sg()
                P.act(t1, U(tt * 512 + 2, 512), AF.Copy, scale=kc(2, m))
                P.stt(t1, U(tt * 512 + 1, 512), kc(1, m), t1, ALU.mult, ALU.add)
                P.stt(t1, U(tt * 512, 512), kc(0, m), t1, ALU.mult, ALU.add)
                P.tt(t.hid(m), t1, pb, ALU.mult)
            if seg == 0:
                P.copy(self.UH.s(m * 2, 2), U(1024, 2))
            self.drain(3)
        self.nring = 4
        yield
        self.proj_out(tiles, lambda k, t: t.hid(k), 8, lambda m: wout_ap[m],
                      lambda m: self.gcol(0, 3, m))

    def qk_proj(self, tiles, seg, w_ap, dst_ap, after_hl=None):
        P = self.P
        for hi in range(8):
            hl, r = hi // 2, hi % 2
            hd = HEAD_GROUPS[r][hl]
            w = self.wload(w_ap[hd], 1024)
            for t in tiles:
                pk = self.bank().s(0, 512)
                P.mm(pk, [(w(c * 128, 128), t.xn(c)) for c in range(8)])
                slot = self.ksi % 3
                self.ksi += 1
                ksb = self.KSB.s(slot * 512, 512)
                P.act(ksb, pk, AF.Copy)
                c0 = seg * SEG + t.tt * 512
                P.dma_out("sp", dst_ap(r, hl)[:, c0:c0 + 512], ksb, "ks%d" % slot, final=False,
                          dram=P.dreg("xs", hl * 2 + r))
            self.drain(3)
            if after_hl is not None and r == 1:
                after_hl(hl)


PAIRS = [[0, 1], [2, 3], [4, 5], [6, 7]]


def build_F():
    K = Kern("F")
    P = K.P
    nc = K.nc
    xT = K.din("xT", [128, NSEG * 8 * SEG])
    xh = K.din("xh", [128, 16])
    gains = K.din("gains", [128, 128])
    cb = K.din("cb", [128, 512], BF16)
    cb2 = K.din("cb2", [128, 384], BF16)
    oh_in = K.din("oh", [128, 2])
    wgu = K.din("wgu", [4, NJ, 128, 2048])
    wdn = K.din("wdn", [4, 8, 128, DFF])
    win = K.din("win", [8, 128, 3072])
    wout = K.din("wout", [8, 128, 1024])
    wk = K.din("wk", [8, 128, 1024])
    wv = K.din("wv", [4, 128, 2048])
    wq = K.din("wq", [8, 128, 1024])
    wo = K.din("wo", [8, 128, 1024])
    alibi = K.din("alibi", [128, 4 * NBIAS])
    tq = K.din("tq", [4, S], BF16)
    lamp = K.din("lamp", [128, 256])
    subg = K.din("subg", [128, 1])
    rank_in = K.din("rank", [1, 1], mybir.dt.int32)
    out = K.dout("out", [128, NSEG * 8 * SEG])
    xs = nc.dram_tensor("xs", [2 * 3 * 4 * 128, TOK], BF16)
    xg = nc.dram_tensor("xg", [2 * 2 * 3 * 4 * 128, TOK], BF16)
    os_ = nc.dram_tensor("os", [2 * 4 * 128, TOK], BF16)
    og = nc.dram_tensor("og", [2 * 2 * 4 * 128, TOK], BF16)
    xs5 = xs.ap().rearrange("(h r k p) t -> h r k p t", h=4, r=2, k=3)
    kt_o = lambda r, hl: xs5[hl, r, 0]
    qt_o = lambda r, hl: xs5[hl, r, 1]
    v_o = lambda r: xs5[:, r, 2].rearrange("h p (b e) -> h p b e", b=16)
    xg6 = xg.ap().rearrange("(h r s k p) t -> h r s k p t", h=4, r=2, s=2, k=3)
    os4 = os_.ap().rearrange("(h d p) t -> h d p t", h=4, d=2)
    og5 = og.ap().rearrange("(h s d p) t -> h s d p t", h=4, s=2, d=2)

    def sp_setup(e, env):
        reg = e.alloc_register("rank")
        e.reg_load(reg, rank_in[0:1, 0:1])
        env["rank"] = e.snap(reg, min_val=0, max_val=1)
    P.sp_setup = sp_setup

    XH = P.sbuf("XH", 19456, F32)
    K.H = P.sbuf("H", NSEG * 8 * SEG, F32)
    K.XY = SubBuf(XH, 0, F32)
    AUX = P.sbuf("AUX", 13832, BF16)
    K.XN = SubBuf(AUX, 0, BF16)
    K.HID = SubBuf(XH, 32768, BF16)
    K.W = P.sbuf("W", NWSLOT * WSLOT, BF16)
    K.SQ = P.sbuf("SQ", 4 * 512, BF16)
    K.SG = P.sbuf("SG", 4 * 512, F32)
    K.RS = P.sbuf("RS", 4 * 512, F32)
    K.CB = P.sbuf("CB", 512, BF16)
    K.G = P.sbuf("G", 136, F32)
    K.G5 = P.sbuf("G5", 32, F32)
    PP = [P.psum("pp%d" % i, 1024) for i in range(4)]
    K.PS = [SubBuf(PP[i // 2], (i % 2) * 2048, F32) for i in range(8)]
    K.ONES_D = K.CB.s(0, 128)
    K.ONES_H = K.CB.s(128, 128)
    K.ONES1 = K.CB.s(256, 128)
    K.IDENT = K.CB.s(384, 128)
    K.U = SubBuf(AUX, 16384, F32)
    K.UH = P.sbuf("UH", 16, F32)
    K.HH = P.sbuf("HH", 16, F32)
    K.XNH = P.sbuf("XNH", 16, BF16)
    K.YH = P.sbuf("YH", 16, F32)
    K.HIDH = P.sbuf("HIDH", NJ * 2, BF16)
    K.RSH = P.sbuf("RSH", 2, F32)
    K.KSB = SubBuf(AUX, 24592, BF16)
    K.EPSB = P.sbuf("EPSB", 1, F32)
    K.ksi = 0
    RAW = [SubBuf(XH, 0, BF16), SubBuf(AUX, 0, BF16)]
    KZ = [SubBuf(XH, 24576, BF16), SubBuf(XH, 32768, BF16)]
    QZ = [SubBuf(XH, 40960, BF16), SubBuf(XH, 49152, BF16)]
    PR = SubBuf(XH, 57344, BF16)
    AT = SubBuf(XH, 63488, F32)
    OB = SubBuf(XH, 71680, BF16)
    CB2 = P.sbuf("CB2", 384, BF16)
    OH = P.sbuf("OH", 8, F32)
    AL = SubBuf(K.RS, 0, F32)
    BI = SubBuf(K.RS, 576, F32)
    LM = SubBuf(K.RS, 1152, F32)
    SUBG = SubBuf(K.RS, 2752, F32)
    MX = SubBuf(K.RS, 2816, F32)
    MASKNEG = CB2.s(0, 128)
    E = [CB2.s(128, 128), CB2.s(256, 128)]
    PS = K.PS
    SQ = K.SQ
    ONES_H, ONES1, IDENT = K.ONES_H, K.ONES1, K.IDENT

    P.memset(K.EPSB.s(0, 1), EPS)
    K.EPSC = K.EPSB.s(0, 1)
    K.load_consts(gains[:, 0:128], cb)
    xops = []
    x4 = xT.rearrange("p (s c t) -> p s c t", s=NSEG, c=8)
    for seg in range(NSEG):
        for tt in range(2):
            if seg == 1 and tt == 0:
                P.dma_in("sp", K.HH.s(0, 16), xh, "xh")
            dstr = K.H.s(seg * 8 * SEG, 8 * SEG)
            dst3 = R(dstr.ap.rearrange("p (c t) -> p c t", c=8)[:, :, tt * 512:(tt + 1) * 512], dstr.buf,
                     dstr.lo, dstr.hi)
            srcx = x4[:, seg, :, tt * 512:(tt + 1) * 512]

            def fn(e, dst3=dst3, srcx=srcx):
                return e.dma_start(out=dst3.ap, in_=srcx)
            op = P.add("sp", fn, [], [K.H.s((seg * 8 + c) * SEG + tt * 512, 512) for c in range(8)],
                       dma_key="x%d" % (seg * 2 + tt))
            if seg == 1:
                op.deps += xops
            else:
                xops.append(op)
    P.dma_in("sp", CB2.s(0, 384), cb2, "c1")
    P.dma_in("sp", OH.s(0, 2), oh_in, "c6")

    TL = [[Tile(K, seg, 0), Tile(K, seg, 1)] for seg in range(NSEG)]
    HALO = [Tile(K, 0, 0, halo=True), None]

    def st_ffn0(seg):
        return K.ffn(([HALO[seg]] if HALO[seg] else []) + TL[seg], 0, wgu, wdn)

    def st_conv(seg):
        return K.conv_mixer(seg, TL[seg], HALO[seg], win, wout)

    def st_ffn1(seg):
        return K.ffn(TL[seg], 1, wgu, wdn)

    def st_kv(seg):
        tiles = TL[seg]
        K.prenorm(tiles, lambda c: K.G.s(12 * 8 + c, 1))
        yield
        K.qk_proj(tiles, seg, wk, kt_o)
        for n4 in range(4):
            w = K.wload(wv[n4], 2048)
            for tb in range(8):
                t = tiles[tb // 4]
                pv = K.bank().s(0, 256)
                P.mm(pv, [(K.XN.s(c * SEG + t.tt * 512 + (tb % 4) * 128, 128), w(c * 256, 256))
                          for c in range(8)])
                P.act(K.HID.s(tb * 1024 + n4 * 256, 256), pv, AF.Copy)
        for tb in range(8):
            for r in range(2):
                src = K.HID.s(tb * 1024 + r * 512, 512)
                dst = v_o(r)[:, :, seg * 8 + tb, :].rearrange("h p e -> p h e")
                src3 = R(src.ap.rearrange("p (h e) -> p h e", h=4), src.buf, src.lo, src.hi)
                op = P.dma_out("sp", dst, src3, "vs%d" % (tb * 2 + r), final=False, dram=P.dreg("xs", r))
                for hl in range(1, 4):
                    P.acc.setdefault("dram_xs", {"w": [], "r": []})["w"].append(
                        (P.dreg("xs", hl * 2 + r).lo, P.dreg("xs", hl * 2 + r).hi, op))
        yield

    def st_ffn2(seg):
        return K.ffn(TL[seg], 2, wgu, wdn)

    def st_q(seg):
        K.prenorm(TL[seg], lambda c: K.gcol(1, 2, c))
        yield
        def exch1(hl):
            for ci in (hl * 2, hl * 2 + 1):
                P.collective("AllGather", PAIRS, xs.ap()[ci * 384:(ci + 1) * 384, :].opt(),
                             xg.ap()[ci * 768:(ci + 1) * 768, :].opt(), P.dreg("xs", ci, whole=True),
                             P.dreg("xg", ci, whole=True), "ccx%d" % ci)

        def exch(hl):
            if hl >= 1:
                exch1(hl - 1)
            if hl == 3:
                exch1(3)
        K.qk_proj(TL[seg], seg, wq, qt_o, after_hl=(exch if seg == NSEG - 1 else None))
        yield

    def run_pipeline(stages):
        gens = [st(seg) for st in stages for seg in range(NSEG)]
        next(gens[0])
        for i, g in enumerate(gens):
            next(g)
            K.drain()
            if i + 1 < len(gens):
                next(gens[i + 1])
            for _ in g:
                pass
        K.drain()

    run_pipeline((st_ffn0, st_conv, st_ffn1, st_kv, st_ffn2, st_q))

    P.dma_in("sp", AL.s(0, 4 * NBIAS), alibi, "c3")
    P.dma_in("sp", LM.s(0, 256), lamp, "c4")
    P.dma_in("sp", SUBG.s(0, 1), subg, "c5")
    P.memset(KZ[0].s(0, S, None, 64, 128), 0.0)
    P.memset(KZ[0].s(0, S, None, 64, 65), 1.0)
    P.memset(KZ[1].s(0, S, None, 0, 64), 0.0)
    P.memset(KZ[1].s(0, S, None, 0, 1), 1.0)
    P.memset(QZ[0].s(0, S, None, 64, 128), 0.0)
    P.memset(QZ[1].s(0, S, None, 0, 64), 0.0)
    for i in range(2):
        P.tt(LM.s(256 + i * 64, 64), LM.s(i * 128, 64), LM.s(i * 128 + 64, 64), ALU.mult)

        def fn(e, i=i):
            return e.reduce_sum(out=LM.s(384 + i, 1).ap, in_=LM.s(256 + i * 64, 64).ap, axis=AX.X)
        P.add("dve", fn, [LM.s(256 + i * 64, 64)], [LM.s(384 + i, 1)])
        P.act(LM.s(386 + i, 1), LM.s(384 + i, 1), AF.Exp)
    P.tt(LM.s(388, 1), LM.s(387, 1), LM.s(386, 1), ALU.subtract)
    P.ts(LM.s(389, 1), LM.s(388, 1), -LAMBDA_INIT, None, ALU.add)
    NEGLAM = LM.s(389, 1)
    P.ts(SUBG.s(1, 1), SUBG.s(0, 1), 1.0 - LAMBDA_INIT, None, ALU.mult)
    SUBG2 = SUBG.s(1, 1)

    sqi = [0]
    sbi = [0]
    MSQ = SubBuf(XH, 73728, BF16)

    def head_load(hl):
        def srcfn(env, hl=hl):
            return xg6[hl, bass.ds(env["rank"], 1)].rearrange("o s k p t -> (o p) (s k) t")
        dstr = RAW[hl % 2].s(0, 3 * S)
        dst4 = R(dstr.ap.rearrange("p (sk t) -> p sk t", sk=6), dstr.buf, dstr.lo, dstr.hi)
        P.dma_in("sp", dst4, srcfn, "kv%d" % (hl % 2), dram=P.dreg("xg", hl * 2, 2, whole=True))

    EBD = E[0]
    head_load(0)
    for hl in range(4):
        slot = 0
        KV = RAW[hl % 2]
        if hl + 1 < 4:
            head_load(hl + 1)
        P.dma_in("sp", QZ[0].s(0, S, None, 64, 65), tq[hl:hl + 1, :], "tq0")
        P.dma_in("sp", QZ[1].s(0, S, None, 0, 1), tq[hl:hl + 1, :], "tq1")

        def kcol(kb):
            return (kb // 16) * 3 * TOK + (kb % 16) * 128

        def qcol(g):
            return (g // 4) * 3 * TOK + TOK + (g % 4) * 512

        def vcol(kb):
            return (kb // 16) * 3 * TOK + 2 * TOK + (kb % 16) * 128

        def bound_ops(h2):
            RW = RAW[h2 % 2]
            mo = (h2 % 2) * 24
            ops = []
            pend = []

            def op_all():
                tiles16 = [(xi, g) for xi in range(2) for g in range(8)]
                for t in range(18):
                    if t < 16:
                        xi, g = tiles16[t]
                        sq = MSQ.s((sqi[0] % 4) * 512, 512)
                        sqi[0] += 1
                        srcr = RW.s(qcol(g) if xi == 0 else kcol(4 * g), 512)
                        P.tt(sq, srcr, srcr, ALU.mult)
                        pend.append((xi, g, sq))
                    if t >= 2:
                        xi2, g2, sq2 = pend.pop(0)
                        bk = PS[sbi[0] % 4].s(0, 512)
                        sbi[0] += 1
                        P.mm(bk, [(EBD, sq2)])
                        P.rmax(MX.s(mo + xi2 * 8 + g2, 1), bk)
            ops.append(op_all)

            def fin():
                for xi in range(2):
                    P.rmax(MX.s(mo + 16 + xi, 1), MX.s(mo + xi * 8, 8))
                P.tt(MX.s(mo + 18, 1), MX.s(mo + 16, 1), MX.s(mo + 17, 1), ALU.mult)
                P.act(MX.s(mo + 19, 1), MX.s(mo + 18, 1), AF.Ln, scale=1.0 / 64.0)
                P.act(MX.s(mo + 19, 1), MX.s(mo + 19, 1), AF.Exp, scale=0.5)
                msel = MSQ.s((sqi[0] % 4) * 512, 2)
                sqi[0] += 1
                P.ts(msel, OH.s(0, 2), MX.s(mo + 19, 1), None, ALU.mult)
                bk = PS[sbi[0] % 4].s(0, 2)
                sbi[0] += 1
                P.mm(bk, [(ONES1, msel)])
                P.copy(MX.s(mo + 20, 2), bk)
                P.tt(MX.s(mo + 22, 1), MX.s(mo + 20, 1), MX.s(mo + 21, 1), ALU.max)
                P.ts(BI.s((h2 % 2) * NBIAS, NBIAS), AL.s(h2 * NBIAS, NBIAS), MX.s(mo + 22, 1), None, ALU.subtract)
            ops.append(fin)
            return ops

        for op in bound_ops(hl):
            op()
        slot = hl % 2
        mq = []

        for g in range(8):
            for c in range(2):
                p0, p1 = c * 64, c * 64 + 64
                P.copy(KZ[c].s(g * 512, 512, None, p0, p1), KV.s(kcol(4 * g), 512, None, p0, p1))
                P.act(QZ[c].s(g * 512, 512, None, p0, p1), KV.s(qcol(g), 512, None, p0, p1), AF.Copy)

        steps = [(g, kb) for g in range(8) for kb in range(4 * g + 4)]
        Ob = [PS[4], PS[5]]
        Lb = [PS[6], PS[7]]

        def emit_score(i):
            g, kb = steps[i]
            j = kb - 4 * g
            cs = max(j, 0) * 128
            ncol = 512 - cs
            sp = PP[i % 2]
            ops = []
            reads = []
            for c in range(2):
                kop = KZ[c].s(kb * 128, 128)
                qop = QZ[c].s(g * 512 + cs, ncol)
                reads += [kop, qop]
                ops.append((sp.s(c * 512 + cs, ncol), kop, qop, sp.s(c * 512 + cs, 128)))
            if j >= 0:
                reads += [IDENT, MASKNEG]

            def fn(e, ops=ops, j=j):
                ins = None
                for so, kop, qop, sm in ops:
                    ins = e.matmul(so.ap, lhsT=kop.ap, rhs=qop.ap, start=True, stop=(j < 0))
                    if j >= 0:
                        ins = e.matmul(sm.ap, lhsT=IDENT.ap, rhs=MASKNEG.ap, start=False, stop=True)
                return ins
            P.add("pe", fn, reads, [sp.s(0, 1024)])

        deferred = []

        def do_pv(ip, icur):
            g, kb = steps[ip]
            nkb = 4 * g + 4
            j = kb - 4 * g
            cs = max(j, 0) * 128
            ncol = 512 - cs
            last = (kb == nkb - 1)
            vop = KV.s(vcol(kb), 128)
            for c in range(2):
                pr = PR.s((ip % 3) * 1024 + c * 512 + cs, ncol)
                P.mm(Ob[c].s(cs, ncol), [(vop, pr)], start=(kb == 0), stop=last)
                P.mm(Lb[c].s(cs, ncol), [(ONES1, pr)], start=(kb == 0), stop=last)
            if not last:
                return
            AT4 = [(AT if g % 2 == 0 else K.SG).s(k * 512, 512) for k in range(4)]
            a0, a1, a2, a3 = AT4
            P.copy(a0, Lb[0].s(0, 512))
            P.act(a1, Ob[0].s(0, 512), AF.Copy)
            P.copy(a2, Lb[1].s(0, 512))
            P.act(a3, Ob[1].s(0, 512), AF.Copy)
            P.recip(a0, a0)
            P.tt(a1, a1, a0, ALU.mult)
            P.recip(a2, a2)
            P.tt(a3, a3, a2, ALU.mult)
            P.stt(a1, a3, NEGLAM, a1, ALU.mult, ALU.add)
            sq = MSQ.s((sqi[0] % 4) * 512, 512)
            sqi[0] += 1
            P.tt(sq, a1, a1, ALU.mult)

            def tail(icur, g=g, a0=a0, a1=a1, sq=sq):
                bk = PP[(icur + 1) % 2].s(0, 512)
                P.mm(bk, [(ONES_H, sq)])
                P.act(a0, bk, AF.Ln, bias=K.EPSC, scale=1.0)
                P.act(a0, a0, AF.Exp, scale=-0.5)
                ob = OB.s((g % 2) * 512, 512)
                P.stt(ob, a1, SUBG2, a0, ALU.mult, ALU.mult)
                P.dma_out("sp", os4[hl, g // 4, :, (g % 4) * 512:(g % 4) * 512 + 512], ob, "ob%d" % (g % 2),
                          final=False, dram=P.dreg("os", hl))
            deferred.append((icur + 12, tail))

        emit_score(0)
        for i, (g, kb) in enumerate(steps):
            while deferred and deferred[0][0] <= i:
                deferred.pop(0)[1](i)
            nkb = 4 * g + 4
            j = kb - 4 * g
            cs = max(j, 0) * 128
            ncol = 512 - cs
            last = (kb == nkb - 1)
            spr = PP[i % 2].s(0, 1024)
            prr = PR.s((i % 3) * 1024, 1024)
            in3 = R(spr.ap.rearrange("p (c n) -> p c n", c=2)[:, :, cs:512], spr.buf, spr.lo, spr.hi)
            out3 = R(prr.ap.rearrange("p (c n) -> p c n", c=2)[:, :, cs:512], prr.buf, prr.lo, prr.hi)
            bias = BI.s(slot * NBIAS + (4 * g - kb + 3), 1)
            P.act(out3, in3, AF.Exp, bias=bias, scale=0.125)
            if i + 1 < len(steps):
                emit_score(i + 1)
            if i >= 1:
                do_pv(i - 1, i)
        do_pv(len(steps) - 1, len(steps))
        while deferred:
            deferred.pop(0)[1](len(steps))
        while mq:
            mq.pop(0)()
        P.collective("AllGather", PAIRS, os_.ap()[hl * 256:(hl + 1) * 256, :].opt(),
                     og.ap()[hl * 512:(hl + 1) * 512, :].opt(), P.dreg("os", hl, whole=True),
                     P.dreg("og", hl, whole=True), "cco%d" % hl)

    ogreg = P.dreg("og", 0, 4, whole=True)

    def st_wo(seg):
        def srcfn(env, seg=seg):
            return og5[:, :, bass.ds(env["rank"], 1), :, seg * SEG:(seg + 1) * SEG].rearrange(
                "h s o p t -> (o p) (h s) t")
        dstr = K.HID.s(0, 8 * SEG)
        dst3 = R(dstr.ap.rearrange("p (c t) -> p c t", c=8), dstr.buf, dstr.lo, dstr.hi)
        yield
        P.dma_in("sp", dst3, srcfn, "at", dram=ogreg)
        yield
        K.proj_out(TL[seg], lambda k, t: t.hid(k), 8, lambda m: wo[m], lambda m: K.gcol(1, 3, m))

    def st_ffn3(seg):
        return K.ffn(TL[seg], 3, wgu, wdn)

    run_pipeline((st_wo, st_ffn3))
    for q in range(4):
        n = NSEG * 8 * SEG // 4
        P.dma_out("sp", out[:, q * n:(q + 1) * n], K.H.s(q * n, n), "ho%d" % q)
    nc = P.emit()
    return nc, P


def _tile_cols(w, cols):
    kk = w.shape[0] // 128
    t = w[:, cols].reshape(kk, 128, len(cols)).transpose(1, 0, 2)
    return np.ascontiguousarray(t).reshape(128, -1)


def _prep_weights(ffn_w_gu, ffn_w_down, conv_w_in, conv_w_out, w_kv, attn_w_q, attn_w_o):
    ar = np.arange
    wgu = np.empty((4, NJ, 128, 2048), np.float32)
    wdn = np.empty((4, 8, 128, DFF), np.float32)
    for f in range(4):
        l, i = f // 2, f % 2
        g = ffn_w_gu[l, i]
        for j in range(NJ):
            cols = np.concatenate([ar(j * 128, j * 128 + 128), DFF + ar(j * 128, j * 128 + 128)])
            wgu[f, j] = _tile_cols(g, cols)
        dn = ffn_w_down[l, i]
        for m in range(8):
            wdn[f, m] = _tile_cols(dn, ar(m * 128, m * 128 + 128))
    win = np.empty((8, 128, 3072), np.float32)
    for m in range(8):
        cols = np.concatenate([ar(m * 128, m * 128 + 128), D + ar(m * 128, m * 128 + 128),
                               2 * D + ar(m * 128, m * 128 + 128)])
        win[m] = _tile_cols(conv_w_in[0], cols)
    sq = lambda w: np.stack([_tile_cols(w, ar(m * 128, m * 128 + 128)) for m in range(8)])
    wout = sq(conv_w_out[0])
    wk = sq(w_kv[:, :D])
    wq = sq(attn_w_q[0])
    perm = np.concatenate([D + ar(h * 128, h * 128 + 128) for g in HEAD_GROUPS for h in g])
    perm_o = np.concatenate([ar(HEAD_GROUPS[r][hl] * 128, HEAD_GROUPS[r][hl] * 128 + 128)
                             for hl in range(4) for r in range(2)])
    wo = sq(attn_w_o[0][perm_o])
    wv = np.stack([_tile_cols(w_kv, perm[n * 256:(n + 1) * 256]) for n in range(4)])
    return dict(wgu=wgu, wdn=wdn, win=win, wout=wout, wk=wk, wq=wq, wo=wo, wv=wv)


def _const_cb():
    cb = np.zeros((128, 512), np.float32)
    cb[:, 0:128] = 1.0 / D
    cb[:, 128:256] = 1.0 / 128.0
    cb[:, 256:384] = 1.0
    cb[:, 384:512] = np.eye(128, dtype=np.float32)
    return cb.astype(ml_dtypes.bfloat16)


def _const_cb2():
    cb2 = np.zeros((128, 384), np.float32)
    k = np.arange(128)[:, None]
    q = np.arange(128)[None, :]
    cb2[:, 0:128] = np.where(k > q, -30000.0, 0.0)
    cb2[0:64, 128:192] = 1.0
    cb2[64:128, 192:256] = 1.0
    cb2[64:128, 256:384] = 1.0
    return cb2.astype(ml_dtypes.bfloat16)


_CACHE = {}


def _get(name, fn):
    if name not in _CACHE:
        _CACHE[name] = fn()
    return _CACHE[name]


def make_maps(x, ffn_w_gu, ffn_w_down, norm_g, conv_w_in, conv_k, conv_w_out, kv_norm_g, w_kv,
              attn_w_q, attn_lambda, attn_subln_g, attn_w_o, cores=range(8)):
    f32 = np.float32
    x = np.asarray(x, f32)
    Wt = _prep_weights(np.asarray(ffn_w_gu, f32), np.asarray(ffn_w_down, f32), np.asarray(conv_w_in, f32),
                       np.asarray(conv_w_out, f32), np.asarray(w_kv, f32), np.asarray(attn_w_q, f32),
                       np.asarray(attn_w_o, f32))
    gains = np.zeros((128, 128), f32)
    gains[:, 0:96] = np.asarray(norm_g, f32).reshape(2, 6, 8, 128).transpose(3, 0, 1, 2).reshape(128, 96)
    gains[:, 96:104] = np.asarray(kv_norm_g, f32).reshape(8, 128).T
    gains[:, 104:128] = np.asarray(conv_k, f32)[0].reshape(3, 8, 128).transpose(2, 0, 1).reshape(128, 24)
    cb = _const_cb()
    cb2 = _const_cb2()
    slopes = 2.0 ** (-8.0 * np.arange(1, 9) / 8.0)
    lam_in = np.tile(np.asarray(attn_lambda, f32)[0].reshape(1, 256), (128, 1))
    subg = np.asarray(attn_subln_g, f32)[0].reshape(128, 1)
    ki = np.arange(128, dtype=np.float64)[:, None]
    npr = np.arange(NBIAS, dtype=np.float64)[None, :] - 3.0
    oh = np.zeros((128, 2), f32)
    oh[0, 0] = 1.0
    oh[64, 1] = 1.0
    maps = []
    for c in cores:
        b, half = c // 2, c % 2
        xs = x[b, half * TOK:(half + 1) * TOK]
        xT = xs.reshape(NSEG, SEG, 8, 128).transpose(3, 0, 2, 1).reshape(128, NSEG * 8 * SEG)
        if half == 0:
            xh = np.zeros((128, 16), f32)
        else:
            xh = x[b, TOK - 2:TOK].reshape(2, 8, 128).transpose(2, 1, 0).reshape(128, 16)
        al = np.zeros((128, 4, NBIAS), f32)
        tq = np.zeros((4, S), f32)
        for hl, hd in enumerate(HEAD_GROUPS[half]):
            al[:, hl, :] = slopes[hd] * (ki - 128.0 * npr)
            tq[hl, :] = -8.0 * slopes[hd] * (np.arange(S) % 512)
        maps.append(dict(xT=np.ascontiguousarray(xT), xh=np.ascontiguousarray(xh), gains=gains, cb=cb, cb2=cb2,
                         wgu=Wt["wgu"], wdn=Wt["wdn"], win=Wt["win"], wout=Wt["wout"], wk=Wt["wk"],
                         wv=Wt["wv"], wq=Wt["wq"], wo=Wt["wo"], alibi=al.reshape(128, 4 * NBIAS),
                         tq=tq.astype(ml_dtypes.bfloat16), lamp=lam_in, subg=subg,
                         rank=np.array([[half]], np.int32), oh=oh))
    return maps


def kernel(x, ffn_w_gu, ffn_w_down, norm_g, conv_w_in, conv_k, conv_w_out, kv_norm_g, w_kv,
           attn_w_q, attn_lambda, attn_subln_g, attn_w_o):
    maps = make_maps(x, ffn_w_gu, ffn_w_down, norm_g, conv_w_in, conv_k, conv_w_out, kv_norm_g, w_kv,
                     attn_w_q, attn_lambda, attn_subln_g, attn_w_o)
    nc, _ = _get("F", build_F)
    res = run_bass_kernel_spmd(nc, maps, core_ids=list(range(8))).results
    out = np.empty((NB, S, D), np.float32)
    for c in range(8):
        b, half = c // 2, c % 2
        o = res[c]["out"].reshape(128, NSEG, 8, SEG).transpose(1, 3, 2, 0).reshape(TOK, D)
        out[b, half * TOK:(half + 1) * TOK] = o
    return out
```
